# Optimizing a Trainium2 kernel written in Bass

```python
import jax, jax.numpy as jnp
from jax import lax
import numpy as np


D_MODEL = 1024
BATCH = 1
SEQ = 16384
DEPTH = 2

HEAD_DIM = 64
A_Q_HEADS = 8
A_KV_HEADS = 2
A_GROUP = A_Q_HEADS // A_KV_HEADS
A_WINDOW = 128
A_BLOCK = 128
B_HEADS = 8
GRID_W = 64
NA_KH = 8
NA_KW = 16
RET_HEADS = 8
RET_DK = D_MODEL // RET_HEADS
RET_DV = 2 * D_MODEL // RET_HEADS
RET_CHUNK = 128
D_FF = 4 * D_MODEL
N_EVEN = (DEPTH + 1) // 2
N_ODD = DEPTH // 2
EVEN_SPLITS = (A_Q_HEADS * HEAD_DIM, A_KV_HEADS * HEAD_DIM, A_KV_HEADS * HEAD_DIM,
               B_HEADS * HEAD_DIM, B_HEADS * HEAD_DIM, B_HEADS * HEAD_DIM)
EVEN_IN = sum(EVEN_SPLITS)
EVEN_MIX = (A_Q_HEADS + B_HEADS) * HEAD_DIM
ODD_SPLITS = (RET_HEADS * RET_DK, RET_HEADS * RET_DK, RET_HEADS * RET_DV, RET_HEADS * RET_DV)
ODD_IN = sum(ODD_SPLITS)
ODD_MIX = RET_HEADS * RET_DV
RMS_EPS = 1e-6
GN_EPS = 1e-5

kernel_name = 'hybrid_swa_natten_retention_encoder'


def _split(t, sizes):
    idx = np.cumsum(sizes)[:-1].tolist()
    return jnp.split(t, idx, axis=-1)


def _rmsnorm(x, g):
    xf = x.astype(jnp.float32)
    y = xf * lax.rsqrt(jnp.mean(xf * xf, axis=-1, keepdims=True) + RMS_EPS)
    return (y * g.astype(jnp.float32)).astype(x.dtype)


def _alibi_slopes(n):
    return 2.0 ** (-(8.0 / n) * (jnp.arange(n, dtype=jnp.float32) + 1.0))


def _window_gqa(q, k, v, sink):
    B, S = q.shape[0], q.shape[1]
    nb = S // A_BLOCK
    qb = q.reshape(B, nb, A_BLOCK, A_KV_HEADS, A_GROUP, HEAD_DIM)
    pad = ((0, 0), (A_BLOCK, A_BLOCK), (0, 0), (0, 0))
    kp = jnp.pad(k, pad).reshape(B, nb + 2, A_BLOCK, A_KV_HEADS, HEAD_DIM)
    vp = jnp.pad(v, pad).reshape(B, nb + 2, A_BLOCK, A_KV_HEADS, HEAD_DIM)
    kb = jnp.concatenate([kp[:, :-2], kp[:, 1:-1], kp[:, 2:]], axis=2)
    vb = jnp.concatenate([vp[:, :-2], vp[:, 1:-1], vp[:, 2:]], axis=2)
    s = jnp.einsum('bnqhgd,bnkhd->bnhgqk', qb, kb).astype(jnp.float32) * (HEAD_DIM ** -0.5)
    qi = jnp.arange(A_BLOCK)[:, None]
    kj = jnp.arange(3 * A_BLOCK)[None, :]
    rel = kj - A_BLOCK - qi
    kpos = jnp.arange(nb)[:, None, None] * A_BLOCK + kj[None] - A_BLOCK
    valid = (jnp.abs(rel) <= A_WINDOW)[None] & (kpos >= 0) & (kpos < S)
    slopes = _alibi_slopes(A_Q_HEADS).reshape(A_KV_HEADS, A_GROUP, 1, 1)
    s = s - slopes * jnp.abs(rel).astype(jnp.float32)
    s = jnp.where(valid[None, :, None, None], s, -jnp.inf)
    sk = sink.astype(jnp.float32).reshape(A_KV_HEADS, A_GROUP)[None, None, :, :, None, None]
    m = jnp.maximum(jnp.max(s, axis=-1, keepdims=True), sk)
    p = jnp.exp(s - m)
    p = p / (jnp.sum(p, axis=-1, keepdims=True) + jnp.exp(sk - m))
    o = jnp.einsum('bnhgqk,bnkhd->bnqhgd', p.astype(v.dtype), vb)
    return o.reshape(B, S, A_Q_HEADS * HEAD_DIM)


def _neighbourhood_attn(q, k, v, rpb):
    B, S = q.shape[0], q.shape[1]
    rows = S // GRID_W
    kh = min(NA_KH, rows)
    kw = NA_KW
    qg = q.reshape(B, rows, GRID_W, B_HEADS, HEAD_DIM)
    kg = k.reshape(B, rows, GRID_W, B_HEADS, HEAD_DIM)
    vg = v.reshape(B, rows, GRID_W, B_HEADS, HEAD_DIM)
    qc = jnp.arange(GRID_W)
    cstart = jnp.clip(qc - kw // 2, 0, GRID_W - kw)
    kc = jnp.arange(GRID_W)
    col_valid = (kc[None, :] >= cstart[:, None]) & (kc[None, :] < cstart[:, None] + kw)
    col_idx = jnp.clip(kc[None, :] - qc[:, None] + NA_KW - 1, 0, 2 * NA_KW - 2)
    rpb_f = rpb.astype(jnp.float32)
    scale = HEAD_DIM ** -0.5

    def row_fn(r):
        rstart = jnp.clip(r - kh // 2, 0, rows - kh)
        qr = lax.dynamic_index_in_dim(qg, r, axis=1, keepdims=False)
        kr = lax.dynamic_slice_in_dim(kg, rstart, kh, axis=1)
        vr = lax.dynamic_slice_in_dim(vg, rstart, kh, axis=1)
        s = jnp.einsum('bqhd,brkhd->bhqrk', qr, kr).astype(jnp.float32) * scale
        row_idx = rstart + jnp.arange(kh) - r + NA_KH - 1
        bias = rpb_f[:, row_idx[None, :, None], col_idx[:, None, :]]
        s = jnp.where(col_valid[:, None, :], s + bias, -jnp.inf)
        p = jax.nn.softmax(s.reshape(B, B_HEADS, GRID_W, kh * GRID_W), axis=-1)
        p = p.reshape(B, B_HEADS, GRID_W, kh, GRID_W).astype(v.dtype)
        return jnp.einsum('bhqrk,brkhd->bqhd', p, vr)

    out = lax.map(row_fn, jnp.arange(rows))
    return out.transpose(1, 0, 2, 3, 4).reshape(B, S, B_HEADS * HEAD_DIM)


def _retention_dir(q, k, v, log_gamma, strict):
    B, S, H, dk = q.shape
    dv = v.shape[-1]
    nc = S // RET_CHUNK
    dt = q.dtype

    def chunks(t):
        return t.reshape(B, nc, RET_CHUNK, H, t.shape[-1]).transpose(1, 0, 3, 2, 4)

    lg = log_gamma[:, None]
    pos = jnp.arange(RET_CHUNK, dtype=jnp.float32)
    diff = pos[:, None] - pos[None, :]
    allowed = (diff > 0) if strict else (diff >= 0)
    dmask = jnp.where(allowed[None], jnp.exp(lg[:, :, None] * jnp.maximum(diff, 0.0)[None]), 0.0).astype(dt)
    cross = jnp.exp(lg * (pos + 1.0)).astype(dt)
    zeta = jnp.exp(lg * (RET_CHUNK - 1.0 - pos)).astype(dt)
    chunk_decay = jnp.exp(log_gamma * RET_CHUNK).astype(dt)

    def step(R, inp):
        qc, kc, vc = inp
        a = jnp.einsum('bhid,bhjd->bhij', qc, kc) * dmask
        o = jnp.einsum('bhij,bhjv->bhiv', a, vc) + jnp.einsum('bhid,bhdv->bhiv', qc, R) * cross[:, :, None]
        R = R * chunk_decay[:, None, None] + jnp.einsum('bhjd,bhjv->bhdv', kc * zeta[:, :, None], vc)
        return R, o

    R0 = jnp.zeros((B, H, dk, dv), dt)
    _, o = lax.scan(step, R0, (chunks(q), chunks(k), chunks(v)))
    return o.transpose(1, 0, 3, 2, 4).reshape(B, S, H, dv)


def _even_mixer(x, norm_g, w_in, qn_a, kn_a, sink_a, qn_b, kn_b, rpb_b, w_out):
    B, S, _ = x.shape
    h = _rmsnorm(x, norm_g)
    qa, ka, va, qb, kb, vb = _split(h @ w_in, EVEN_SPLITS)
    qa = _rmsnorm(qa.reshape(B, S, A_KV_HEADS, A_GROUP, HEAD_DIM), qn_a)
    ka = _rmsnorm(ka.reshape(B, S, A_KV_HEADS, HEAD_DIM), kn_a)
    va = va.reshape(B, S, A_KV_HEADS, HEAD_DIM)
    oa = _window_gqa(qa, ka, va, sink_a)
    qb = _rmsnorm(qb.reshape(B, S, B_HEADS, HEAD_DIM), qn_b)
    kb = _rmsnorm(kb.reshape(B, S, B_HEADS, HEAD_DIM), kn_b)
    vb = vb.reshape(B, S, B_HEADS, HEAD_DIM)
    ob = _neighbourhood_attn(qb, kb, vb, rpb_b)
    return jnp.concatenate([oa, ob], axis=-1) @ w_out


def _odd_mixer(x, norm_g, w_in, dec_f, dec_b, gn_g, w_out):
    B, S, _ = x.shape
    h = _rmsnorm(x, norm_g)
    q, k, v, g = _split(h @ w_in, ODD_SPLITS)
    q = q.reshape(B, S, RET_HEADS, RET_DK)
    k = k.reshape(B, S, RET_HEADS, RET_DK) * (RET_DK ** -0.5)
    v = v.reshape(B, S, RET_HEADS, RET_DV)
    lgf = jax.nn.log_sigmoid(dec_f.astype(jnp.float32))
    lgb = jax.nn.log_sigmoid(dec_b.astype(jnp.float32))
    y_f = _retention_dir(q, k, v, lgf, False)
    y_b = jnp.flip(_retention_dir(jnp.flip(q, 1), jnp.flip(k, 1), jnp.flip(v, 1), lgb, True), 1)
    yf = (y_f + y_b).astype(jnp.float32)
    mu = jnp.mean(yf, axis=-1, keepdims=True)
    var = jnp.mean(jnp.square(yf - mu), axis=-1, keepdims=True)
    yn = (yf - mu) * lax.rsqrt(var + GN_EPS) * gn_g.astype(jnp.float32).reshape(RET_HEADS, RET_DV)
    yn = yn.astype(x.dtype).reshape(B, S, ODD_MIX)
    return (jax.nn.silu(g) * yn) @ w_out


def _mlp(x, norm_g, w1, w2):
    h = _rmsnorm(x, norm_g)
    return jnp.square(jax.nn.relu(h @ w1)) @ w2


def setup_inputs(seed: int = 0) -> dict:
    key = jax.random.key(seed)
    ks = jax.random.split(key, 20)
    f32 = jnp.float32

    def nrm(k, shape, scale):
        return jax.random.normal(k, shape, f32) * scale

    def gain(k, shape):
        return 1.0 + 0.02 * jax.random.normal(k, shape, f32)

    hr = jnp.arange(RET_HEADS, dtype=f32)
    gamma = 1.0 - 2.0 ** (-5.0 - hr)
    base_logit = jnp.log(gamma) - jnp.log1p(-gamma)
    return {
        'x': nrm(ks[0], (BATCH, SEQ, D_MODEL), 1.0),
        'attn_norm_e': gain(ks[1], (N_EVEN, D_MODEL)),
        'w_in_e': nrm(ks[2], (N_EVEN, D_MODEL, EVEN_IN), D_MODEL ** -0.5),
        'q_norm_a': gain(ks[3], (N_EVEN, HEAD_DIM)),
        'k_norm_a': gain(ks[4], (N_EVEN, HEAD_DIM)),
        'sink_a': nrm(ks[5], (N_EVEN, A_Q_HEADS), 0.5),
        'q_norm_b': gain(ks[6], (N_EVEN, HEAD_DIM)),
        'k_norm_b': gain(ks[7], (N_EVEN, HEAD_DIM)),
        'rpb_b': nrm(ks[8], (N_EVEN, B_HEADS, 2 * NA_KH - 1, 2 * NA_KW - 1), 0.02),
        'w_out_e': nrm(ks[9], (N_EVEN, EVEN_MIX, D_MODEL), EVEN_MIX ** -0.5),
        'ret_norm_o': gain(ks[10], (N_ODD, D_MODEL)),
        'w_in_o': nrm(ks[11], (N_ODD, D_MODEL, ODD_IN), D_MODEL ** -0.5),
        'decay_fwd_o': base_logit[None] + nrm(ks[12], (N_ODD, RET_HEADS), 0.05),
        'decay_bwd_o': base_logit[None] + nrm(ks[13], (N_ODD, RET_HEADS), 0.05),
        'ret_gn_o': gain(ks[14], (N_ODD, ODD_MIX)),
        'w_out_o': nrm(ks[15], (N_ODD, ODD_MIX, D_MODEL), ODD_MIX ** -0.5),
        'mlp_norm': gain(ks[16], (DEPTH, D_MODEL)),
        'w_mlp_in': nrm(ks[17], (DEPTH, D_MODEL, D_FF), D_MODEL ** -0.5),
        'w_mlp_out': nrm(ks[18], (DEPTH, D_FF, D_MODEL), D_FF ** -0.5),
    }


def reference(x, attn_norm_e, w_in_e, q_norm_a, k_norm_a, sink_a, q_norm_b, k_norm_b, rpb_b, w_out_e,
              ret_norm_o, w_in_o, decay_fwd_o, decay_bwd_o, ret_gn_o, w_out_o,
              mlp_norm, w_mlp_in, w_mlp_out):
    for layer in range(DEPTH):
        i = layer // 2
        if layer % 2 == 0:
            x = x + _even_mixer(x, attn_norm_e[i], w_in_e[i], q_norm_a[i], k_norm_a[i], sink_a[i],
                                q_norm_b[i], k_norm_b[i], rpb_b[i], w_out_e[i])
        else:
            x = x + _odd_mixer(x, ret_norm_o[i], w_in_o[i], decay_fwd_o[i], decay_bwd_o[i],
                               ret_gn_o[i], w_out_o[i])
        x = x + _mlp(x, mlp_norm[layer], w_mlp_in[layer], w_mlp_out[layer])
    return x
```

```python
import contextlib
import numpy as np
import concourse.bass as bass
import concourse.mybir as mybir

F32 = mybir.dt.float32
BF16 = mybir.dt.bfloat16
ALU = mybir.AluOpType
AF = mybir.ActivationFunctionType
AX = mybir.AxisListType

SAME_ENGINE_SYNC = True


class Op:
    __slots__ = ("eng", "fn", "waits", "signal", "sigval", "seq", "dma", "stream", "dmaval")

    def __init__(self, eng, fn):
        self.eng = eng
        self.fn = fn
        self.waits = []
        self.signal = False
        self.sigval = None
        self.seq = None
        self.dma = False
        self.stream = None
        self.dmaval = None


class Prog:
    ENGS = ("pe", "act", "dve", "pool", "sp")

    def __init__(self, nc):
        self.nc = nc
        self.ops = {e: [] for e in self.ENGS}
        self.lastw = {}
        self.readers = {}
        self.streams = {}
        self.waited = {e: {} for e in self.ENGS}
        self.stack = contextlib.ExitStack()
        self.stacks = [self.stack]
        self.n_alloc = 0
        self.pending = {e: None for e in self.ENGS}

    @contextlib.contextmanager
    def scope(self):
        st = contextlib.ExitStack()
        self.stacks.append(st)
        try:
            yield
        finally:
            self.stacks.pop()
            st.close()
            self.barrier()

    def barrier(self):
        last = []
        for e in self.ENGS:
            for o in reversed(self.ops[e]):
                if not o.dma:
                    last.append(o)
                    break
        dm = [(s, c[0]) for s, c in self.streams.items()]
        for e in self.ENGS:
            self.pending[e] = (last, dm)

    def sbuf(self, shape, dt, name=None):
        self.n_alloc += 1
        name = f"sb{self.n_alloc}_" + (name or "")
        return self.stacks[-1].enter_context(self.nc.sbuf_tensor(name, list(shape), dt))

    def psum(self, shape, dt, name=None):
        self.n_alloc += 1
        name = f"ps{self.n_alloc}_" + (name or "")
        return self.stack.enter_context(self.nc.psum_tensor(name, list(shape), dt))

    def _dep(self, op, prod):
        if prod is None or prod is op:
            return
        e = op.eng
        if prod.dma:
            k = ("dma", prod.stream)
            if self.waited[e].get(k, 0) >= prod.dmaval:
                return
            self.waited[e][k] = prod.dmaval
            op.waits.append(("dma", prod.stream, prod.dmaval))
            return
        if prod.eng == e:
            if e == "pe" or not SAME_ENGINE_SYNC:
                return
        if self.waited[e].get(prod.eng, -1) >= prod.seq:
            return
        self.waited[e][prod.eng] = prod.seq
        prod.signal = True
        op.waits.append(("op", prod))

    def add(self, eng, fn, reads=(), writes=(), dma_stream=None):
        op = Op(eng, fn)
        op.seq = len(self.ops[eng])
        if dma_stream is not None:
            op.dma = True
            op.stream = dma_stream
            c = self.streams.setdefault(dma_stream, [0])
            c[0] += 16
            op.dmaval = c[0]
        pend = self.pending[eng]
        if pend is not None:
            self.pending[eng] = None
            for prod in pend[0]:
                if prod.dma:
                    continue
                self._dep(op, prod)
            for sname, val in pend[1]:
                k = ("dma", sname)
                if self.waited[eng].get(k, 0) < val:
                    self.waited[eng][k] = val
                    op.waits.append(("dma", sname, val))
        for k in reads:
            self._dep(op, self.lastw.get(k))
        for k in writes:
            self._dep(op, self.lastw.get(k))
            for r in self.readers.get(k, ()):
                self._dep(op, r)
        for k in reads:
            self.readers.setdefault(k, []).append(op)
        for k in writes:
            self.lastw[k] = op
            self.readers[k] = []
        self.ops[eng].append(op)
        return op

    def dma(self, eng, out, in_, reads=(), writes=(), stream="d0", **kw):
        return self.add(eng, lambda E: E.dma_start(out=out, in_=in_, **kw), reads, writes, dma_stream=stream)

    def emit(self, final_waits=()):
        nc = self.nc
        st = self.stack
        semE = {e: st.enter_context(nc.semaphore("s_" + e)) for e in self.ENGS}
        semD = {s: st.enter_context(nc.semaphore("d_" + s)) for s in self.streams}
        for e in self.ENGS:
            n = 0
            for op in self.ops[e]:
                if op.signal and not op.dma:
                    n += 1
                    op.sigval = n
        self.sig_counts = {e: sum(1 for o in self.ops[e] if o.signal and not o.dma) for e in self.ENGS}
        block = st.enter_context(nc.Block())

        def run(eng_name):
            def body(E):
                for op in self.ops[eng_name]:
                    for w in op.waits:
                        if w[0] == "dma":
                            E.wait_ge(semD[w[1]], w[2])
                        else:
                            E.wait_ge(semE[w[1].eng], w[1].sigval)
                    ins = op.fn(E)
                    if op.dma:
                        ins.then_inc(semD[op.stream], 16)
                    elif op.signal:
                        ins.then_inc(semE[eng_name], 1)
                if eng_name == "sp":
                    for s in final_waits:
                        E.wait_ge(semD[s], self.streams[s][0])
            return body

        block.tensor(run("pe"))
        block.scalar(run("act"))
        block.vector(run("dve"))
        block.gpsimd(run("pool"))
        block.sync(run("sp"))

    def close(self):
        self.stack.close()


RMS_EPS = 1e-6


class Ctx:
    def __init__(self, P, NT, ident_dram=None):
        self.P = P
        self.NT = NT
        nc = P.nc
        self.banks = [P.psum([128, 512], F32, name=f"bank{i}") for i in range(8)]
        self.bank_rr = 0
        self.ones_f = P.sbuf([128, 128], F32, name="ones_f")
        P.add("pool", lambda E: E.memset(self.ones_f[:], 1.0 / 1024.0), writes=["ones_f"])
        self.eps_t = P.sbuf([128, 1], F32, name="eps_t")
        self.blockones = P.sbuf([128, 128], F32, name="blockones")
        P.add("pool", lambda E: E.memset(self.blockones[:], 0.0), writes=["blockones"])
        P.add("pool", lambda E: E.memset(self.blockones[0:64, 0:64], 1.0 / 64.0), writes=["blockones"])
        P.add("pool", lambda E: E.memset(self.blockones[64:128, 64:128], 1.0 / 64.0), writes=["blockones"])
        self.ident = P.sbuf([128, 128], BF16, name="ident")
        if ident_dram is not None:
            P.dma("pool", self.ident[:], ident_dram, writes=["ident"], stream="c0")
        P.add("pool", lambda E: E.memset(self.eps_t[:], RMS_EPS), writes=["eps_t"])

    def bank(self):
        i = self.bank_rr
        self.bank_rr = (self.bank_rr + 1) % 8
        return i


def rmsnorm_T(cx, xT, xkey, gcol, hT, hkey, t0, t1, scr, tag):
    P = cx.P
    for b0 in range(t0, t1, 512):
        b1 = min(b0 + 512, t1)
        n = b1 - b0
        sq, acc, rstd = scr["sq"], scr["acc"], scr["rstd"]
        for s0 in range(0, n, 256):
            m = min(256, n - s0)
            P.add("act", lambda E, b0=b0, s0=s0, m=m: E.activation(out=sq[:, :, 0:m], in_=xT[:, :, b0 + s0:b0 + s0 + m], func=AF.Square),
                  reads=[(xkey, b0 // 512)], writes=["sq"])
            P.add("dve", lambda E, s0=s0, m=m: E.reduce_sum(out=acc[:, s0:s0 + m], in_=sq[:, :, 0:m].rearrange("p c t -> p t c"), axis=AX.X),
                  reads=["sq"], writes=["acc"])
        bi = cx.bank()
        bk = cx.banks[bi]
        P.add("pe", lambda E, n=n, bk=bk: E.matmul(bk[:, 0:n], lhsT=cx.ones_f[:], rhs=acc[:, 0:n], start=True, stop=True),
              reads=["acc", "ones_f"], writes=[("bank", bi)])
        P.add("act", lambda E, n=n, bk=bk: E.activation(out=rstd[:, 0:n], in_=bk[:, 0:n], func=AF.Sqrt, bias=cx.eps_t[:, 0:1], scale=1.0),
              reads=[("bank", bi), "eps_t"], writes=["rstd"])
        P.add("dve", lambda E, n=n: E.reciprocal(out=rstd[:, 0:n], in_=rstd[:, 0:n]),
              reads=["rstd"], writes=["rstd"])
        for c in range(8):
            eng = "dve"
            P.add(eng, lambda E, c=c, b0=b0, b1=b1, n=n: E.scalar_tensor_tensor(
                out=hT[:, c, b0 - t0:b1 - t0], in0=xT[:, c, b0:b1], scalar=gcol(c), in1=rstd[:, 0:n],
                op0=ALU.mult, op1=ALU.mult),
                reads=[(xkey, b0 // 512), "rstd", "gains"], writes=[(hkey, (b0 - t0) // 512)])


def mlp_T(cx, layer, xT, w1, w2, gcol, scr):
    P = cx.P
    NT = cx.NT
    W2 = scr["W2"]
    W1b = scr["W1b"]
    aT = scr["aT"]
    hT = scr["hT"]
    rl = scr["rl"]
    w2v = w2.rearrange("(f p) o -> p f o", p=128)
    w1v = w1.rearrange("(c p) f -> p c f", p=128)
    for q in range(8):
        P.dma("pool", W2[:, q * 4:(q + 1) * 4, :], w2v[:, q * 4:(q + 1) * 4, :], writes=[("W2", q)], stream="w2")
    g_i = 0
    for tb in range(NT // 512):
        t0 = tb * 512
        rmsnorm_T(cx, xT, "xT", gcol, hT, "hT", t0, t0 + 512, scr, "mlp")
        for g in range(8):
            wb = W1b[g_i % 2]
            wk = ("W1b", g_i % 2)
            P.dma("pool", wb[:], w1v[:, :, g * 512:(g + 1) * 512], writes=[wk], stream=f"w1_{g_i % 2}")
            g_i += 1
            for j in range(4):
                f = g * 4 + j
                bi = cx.bank()
                bk = cx.banks[bi]
                for c in range(8):
                    P.add("pe", lambda E, c=c, j=j, wb=wb, bk=bk: E.matmul(
                        bk[:, :], lhsT=wb[:, c, j * 128:(j + 1) * 128], rhs=hT[:, c, :], start=(c == 0), stop=(c == 7)),
                        reads=[wk, ("hT", 0)], writes=[("bank", bi)])
                r = rl[f % 2]
                rk = ("rl", f % 2)
                P.add("act", lambda E, r=r, bk=bk: E.activation(out=r[:], in_=bk[:, :], func=AF.Relu),
                      reads=[("bank", bi)], writes=[rk])
                eng = "pool" if f % 2 == 0 else "dve"
                P.add(eng, lambda E, r=r, f=f: E.tensor_tensor(out=aT[:, f, :], in0=r[:], in1=r[:], op=ALU.mult),
                      reads=[rk], writes=[("aT", f)])
        for o in range(8):
            bi = cx.bank()
            bk = cx.banks[bi]
            for f in range(32):
                P.add("pe", lambda E, f=f, o=o, bk=bk: E.matmul(
                    bk[:, :], lhsT=W2[:, f, o * 128:(o + 1) * 128], rhs=aT[:, f, :], start=(f == 0), stop=(f == 31)),
                    reads=[("W2", f // 4), ("aT", f)], writes=[("bank", bi)])
            P.add("dve", lambda E, o=o, t0=t0, bk=bk: E.tensor_tensor(
                out=xT[:, o, t0:t0 + 512], in0=xT[:, o, t0:t0 + 512], in1=bk[:, :], op=ALU.add),
                reads=[("bank", bi), ("xT", tb)], writes=[("xT", tb)])


def rms_seg(cx, src, skey, s0, n, dst, d0, dkeys, gcol, scr):
    P = cx.P
    sq, acc, rstd = scr["sq"], scr["acc"], scr["rstd"]
    for a0 in range(0, n, 256):
        m = min(256, n - a0)
        P.add("act", lambda E, a0=a0, m=m: E.activation(out=sq[:, :, 0:m], in_=src[:, :, s0 + a0:s0 + a0 + m], func=AF.Square),
              reads=[skey], writes=["sq"])
        P.add("dve", lambda E, a0=a0, m=m: E.reduce_sum(out=acc[:, a0:a0 + m], in_=sq[:, :, 0:m].rearrange("p c t -> p t c"), axis=AX.X),
              reads=["sq"], writes=["acc"])
    bi = cx.bank()
    bk = cx.banks[bi]
    P.add("pe", lambda E: E.matmul(bk[:, 0:n], lhsT=cx.ones_f[:], rhs=acc[:, 0:n], start=True, stop=True),
          reads=["acc", "ones_f"], writes=[("bank", bi)])
    P.add("act", lambda E: E.activation(out=rstd[:, 0:n], in_=bk[:, 0:n], func=AF.Sqrt, bias=cx.eps_t[:, 0:1], scale=1.0),
          reads=[("bank", bi), "eps_t"], writes=["rstd"])
    P.add("dve", lambda E: E.reciprocal(out=rstd[:, 0:n], in_=rstd[:, 0:n]), reads=["rstd"], writes=["rstd"])
    for c in range(8):
        P.add("dve", lambda E, c=c: E.scalar_tensor_tensor(
            out=dst[:, c, d0:d0 + n], in0=src[:, c, s0:s0 + n], scalar=gcol(c), in1=rstd[:, 0:n],
            op0=ALU.mult, op1=ALU.mult), reads=[skey, "rstd", "gains"], writes=list(dkeys))


def qknorm(cx, bk, bi, n, out_ap, okeys, gain_ap, scr, par):
    P = cx.P
    sqb = scr["sqb"][par]
    rs = scr["rs"][par]
    P.add("act", lambda E: E.activation(out=sqb[:, 0:n], in_=bk[:, 0:n], func=AF.Square),
          reads=[("bank", bi)], writes=[("sqb", par)])
    b2 = cx.bank()
    bk2 = cx.banks[b2]
    P.add("pe", lambda E: E.matmul(bk2[:, 0:n], lhsT=cx.blockones[:], rhs=sqb[:, 0:n], start=True, stop=True),
          reads=[("sqb", par), "blockones"], writes=[("bank", b2)])
    P.add("act", lambda E: E.activation(out=rs[:, 0:n], in_=bk2[:, 0:n], func=AF.Sqrt, bias=cx.eps_t[:, 0:1], scale=1.0),
          reads=[("bank", b2), "eps_t"], writes=[("rs", par)])
    P.add("dve", lambda E: E.reciprocal(out=rs[:, 0:n], in_=rs[:, 0:n]), reads=[("rs", par)], writes=[("rs", par)])
    P.add("dve", lambda E: E.scalar_tensor_tensor(out=out_ap, in0=bk[:, 0:n], scalar=gain_ap, in1=rs[:, 0:n],
                                                  op0=ALU.mult, op1=ALU.mult),
          reads=[("bank", bi), ("rs", par), "gqk"], writes=list(okeys))


def hkeys(lo, hi):
    return [("hT", j) for j in range(lo // 256, (hi + 255) // 256)]


def proj_feat(cx, hT, t0, ntok, w, wkey, col0, consumer):
    P = cx.P
    for b0 in range(0, ntok, 512):
        n = min(512, ntok - b0)
        bi = cx.bank()
        bk = cx.banks[bi]
        for c in range(8):
            P.add("pe", lambda E, c=c, b0=b0, n=n, bk=bk: E.matmul(
                bk[:, 0:n], lhsT=w[:, c, col0:col0 + 128], rhs=hT[:, c, t0 + b0:t0 + b0 + n], start=(c == 0), stop=(c == 7)),
                reads=(wkey if isinstance(wkey, list) else [wkey]) + hkeys(t0 + b0, t0 + b0 + n), writes=[("bank", bi)])
        consumer(bk, bi, b0, n)


def proj_tok(cx, hT, ntile, w, wkey, col0, ncols, consumer):
    P = cx.P
    for j in range(ntile):
        bi = cx.bank()
        bk = cx.banks[bi]
        for c in range(8):
            P.add("pe", lambda E, c=c, j=j, bk=bk: E.matmul(
                bk[:, 0:ncols], lhsT=hT[:, c, j * 128:(j + 1) * 128], rhs=w[:, c, col0:col0 + ncols], start=(c == 0), stop=(c == 7)),
                reads=(wkey if isinstance(wkey, list) else [wkey]) + hkeys(j * 128, (j + 1) * 128), writes=[("bank", bi)])
        consumer(bk, bi, j)


def outproj_chunk(cx, xT, Obuf, okey, ocol0, ntile, Wrows, wkey, OTm, ident):
    P = cx.P
    for n0 in range(0, ntile, 4):
        bi = cx.bank()
        bkb = cx.banks[bi][:, :].bitcast(BF16)
        for n in range(n0, n0 + 4):
            P.add("pe", lambda E, n=n, n0=n0, bkb=bkb: E.transpose(
                out=bkb[:, (n - n0) * 128:(n - n0 + 1) * 128], in_=Obuf[:, n, ocol0:ocol0 + 128], identity=ident[:]),
                reads=[(okey, n), "ident"], writes=[("bank", bi)])
        eng = "act" if (n0 // 4) % 2 == 0 else "dve"
        if eng == "act":
            P.add("act", lambda E, n0=n0, bkb=bkb: E.copy(out=OTm[:, n0 * 128:(n0 + 4) * 128], in_=bkb[:, 0:512]),
                  reads=[("bank", bi)], writes=[("OTm", n0 // 4)])
        else:
            P.add("dve", lambda E, n0=n0, bkb=bkb: E.tensor_copy(out=OTm[:, n0 * 128:(n0 + 4) * 128], in_=bkb[:, 0:512]),
                  reads=[("bank", bi)], writes=[("OTm", n0 // 4)])
    for o in range(8):
        for tb in range(ntile // 4):
            bi = cx.bank()
            bk = cx.banks[bi]
            P.add("pe", lambda E, o=o, tb=tb, bk=bk: E.matmul(bk[:, :], lhsT=Wrows(o), rhs=OTm[:, tb * 512:(tb + 1) * 512],
                                                              start=True, stop=True),
                  reads=[wkey, ("OTm", tb)], writes=[("bank", bi)])
            P.add("dve", lambda E, o=o, tb=tb, bk=bk: E.tensor_tensor(
                out=xT[:, o, tb * 512:(tb + 1) * 512], in0=xT[:, o, tb * 512:(tb + 1) * 512], in1=bk[:, :], op=ALU.add),
                reads=[("bank", bi), ("xT", tb)], writes=[("xT", tb)])


def attn_in_proj(cx, hT, w, wkey, qcols, kcol, vcol, nh, QT, KT, V, kval, gq, gk, scr, tag):
    P = cx.P
    par = [0]

    def nxt():
        par[0] ^= 1
        return par[0]

    for gi, (col0, dst) in enumerate(qcols):
        proj_feat(cx, hT, 256, 2048, w, wkey, col0,
                  lambda bk, bi, b0, n, dst=dst, gi=gi: qknorm(cx, bk, bi, n, dst(b0, n), [(tag + "QT", gi, b0 // 512)], gq, scr, nxt()))
    proj_feat(cx, hT, 0, 2560, w, wkey, kcol,
              lambda bk, bi, b0, n: qknorm(cx, bk, bi, n, KT[:, b0:b0 + n], [(tag + "KT", b0 // 512)], gk, scr, nxt()))
    Vv = V[:, :, :].rearrange("p j (h e) -> p j h e", e=65)
    for h in range(nh):
        P.add("dve", lambda E, h=h: E.tensor_copy(out=Vv[:, :, h, 64], in_=kval[:, :]), reads=["kval"], writes=[(tag + "Vones", h)])

    def vcons(bk, bi, j):
        P.add("act", lambda E: E.activation(out=Vv[:, j, :, 0:64], in_=bk[:, 0:nh * 64].rearrange("p (h d) -> p h d", d=64),
                                            func=AF.Copy, scale=kval[:, j:j + 1]),
              reads=[("bank", bi), "kval"], writes=[(tag + "V", j)])
    proj_tok(cx, hT, 20, w, wkey, vcol, nh * 64, vcons)


def even_mixer(cx, xT, D):
    P = cx.P
    nc = P.nc
    with P.scope():
        hT = P.sbuf([128, 8, 2560], BF16, "hT0")
        Wout = P.sbuf([128, 8, 1024], BF16, "Wout0")
        Mtab = P.sbuf([128, 3, 8, 128], BF16, "Mtab")
        gqk = P.sbuf([128, 4], F32, "gqk")
        esink = P.sbuf([128, 8], F32, "esink")
        kval = P.sbuf([128, 20], F32, "kval")
        gat = P.sbuf([128, 8], F32, "gat")
        OTm = P.sbuf([128, 2048], BF16, "OTm")
        scr = dict(sqb=[P.sbuf([128, 512], F32, f"sqb{i}") for i in range(2)],
                   rs=[P.sbuf([128, 512], F32, f"rs{i}") for i in range(2)])
        P.dma("sp", gqk[:], D["gqk"], writes=["gqk"], stream="c0")
        P.dma("sp", esink[:], D["sink"], writes=["esink"], stream="c0")
        P.dma("sp", kval[:], D["kval"], writes=["kval"], stream="c0")
        P.dma("sp", gat[:], D["g_attn"].rearrange("(c p) -> p c", p=128), writes=["gains"], stream="c0",
              allow_slow_non_contiguous=True)
        P.add("act", lambda E: E.activation(out=esink[:], in_=esink[:], func=AF.Exp), reads=["esink"], writes=["esink"])
        wov = D["wout"].rearrange("(m p) o -> p m o", p=128)
        for q in range(2):
            P.dma("pool", Wout[:, q * 4:(q + 1) * 4, :], wov[:, q * 4:(q + 1) * 4, :], writes=[("Wout", q)], stream="wout")
        with P.scope():
            xh = P.sbuf([128, 8, 512], F32, "xh")
            rscr = dict(sq=P.sbuf([128, 8, 256], F32, "sq"), acc=P.sbuf([128, 512], F32, "acc"), rstd=P.sbuf([128, 512], F32, "rstd"))
            mst = P.sbuf([128, 3 * 8 * 128], F32, "mst")
            P.dma("sp", xh[:], D["xh"].rearrange("(c p) t -> p c t", p=128), writes=["xh"], stream="c1")
            P.dma("sp", mst[:], D["gqa_tab"], writes=["mst"], stream="c1")
            P.add("act", lambda E: E.activation(out=Mtab[:, :, :, :].rearrange("p a h q -> p (a h q)"), in_=mst[:], func=AF.Exp),
                  reads=["mst"], writes=["Mtab"])
            gcol = lambda c: gat[:, c:c + 1]
            rms_seg(cx, xh, "xh", 0, 256, hT, 0, hkeys(0, 256), gcol, rscr)
            rms_seg(cx, xh, "xh", 256, 256, hT, 2304, hkeys(2304, 2560), gcol, rscr)
            for tb in range(4):
                rms_seg(cx, xT, ("xT", tb), tb * 512, 512, hT, 256 + tb * 512, hkeys(256 + tb * 512, 768 + tb * 512), gcol, rscr)
        winv = D["win"].rearrange("(c p) f -> p c f", p=128)
        winna = D["win"][:, 768:2304].rearrange("(c p) (q f) -> p c q f", p=128, q=3)
        with P.scope():
            w = P.sbuf([128, 8, 768], BF16, "wA")
            QT = P.sbuf([128, 4, 2048], BF16, "QTa")
            KT = P.sbuf([128, 2560], BF16, "KTa")
            V = P.sbuf([128, 20, 130], BF16, "Va")
            Ob = P.sbuf([128, 16, 512], BF16, "ObA")
            Es = [P.sbuf([128, 512], F32, f"EsA{i}") for i in range(3)]
            PT = [P.sbuf([128, 512], BF16, f"PTA{i}") for i in range(3)]
            den = P.sbuf([128, 4], F32, "denA")
            wkeyA = ("wAall",)
            P.dma("pool", w[:, :, :], winv[:, :, 0:768], writes=[wkeyA], stream="wA")
            attn_in_proj(cx, hT, w, wkeyA,
                         [(g * 128, (lambda b0, n, g=g: QT[:, g, b0:b0 + n])) for g in range(4)],
                         512, 640, 2, QT, KT, V, kval, gqk[:, 0:1], gqk[:, 1:2], scr, "A")
            for n in range(16):
                for kv in range(2):
                    lo, hi = kv * 64, (kv + 1) * 64
                    for rp in range(3):
                        j = n + 1 + rp
                        bi = cx.bank()
                        bk = cx.banks[bi]
                        P.add("pe", lambda E, j=j, bk=bk, lo=lo, hi=hi, n=n: E.matmul(
                            bk[:, :], lhsT=KT[lo:hi, j * 128:(j + 1) * 128], rhs=QT[lo:hi, :, n * 128:(n + 1) * 128],
                            start=True, stop=True),
                            reads=[("AKT", j // 4)] + [("AQT", g, n // 4) for g in range(4)], writes=[("bank", bi)])
                        P.add("act", lambda E, rp=rp, bk=bk: E.activation(out=Es[rp][:], in_=bk[:, :], func=AF.Exp, scale=0.125),
                              reads=[("bank", bi)], writes=[("EsA", rp)])
                        eng = "pool" if rp != 1 else "dve"
                        P.add(eng, lambda E, rp=rp, kv=kv: E.tensor_tensor(
                            out=PT[rp][:], in0=Es[rp][:], in1=Mtab[:, rp, kv * 4:(kv + 1) * 4, :].rearrange("p h q -> p (h q)"),
                            op=ALU.mult), reads=[("EsA", rp), "Mtab"], writes=[("PTA", rp)])
                    bo = cx.bank()
                    bko = cx.banks[bo]
                    for g in range(4):
                        for rp in range(3):
                            j = n + 1 + rp
                            P.add("pe", lambda E, g=g, rp=rp, j=j, kv=kv, bko=bko: E.matmul(
                                bko[:, g * 65:(g + 1) * 65], lhsT=PT[rp][:, g * 128:(g + 1) * 128], rhs=V[:, j, kv * 65:(kv + 1) * 65],
                                start=(rp == 0), stop=(rp == 2)),
                                reads=[("PTA", rp), ("AV", j)] + [("AVones", h) for h in range(2)], writes=[("bank", bo)])
                    bov = bko[:, 0:260].rearrange("p (g e) -> p g e", e=65)
                    P.add("dve", lambda E, kv=kv, bov=bov: E.tensor_tensor(out=den[:, :], in0=bov[:, :, 64], in1=esink[:, kv * 4:(kv + 1) * 4],
                                                                           op=ALU.add),
                          reads=[("bank", bo), "esink"], writes=["denA"])
                    P.add("dve", lambda E: E.reciprocal(out=den[:, :], in_=den[:, :]), reads=["denA"], writes=["denA"])
                    for g in range(4):
                        P.add("act", lambda E, g=g, kv=kv, n=n, bko=bko: E.activation(
                            out=Ob[:, n, (kv * 4 + g) * 64:(kv * 4 + g + 1) * 64], in_=bko[:, g * 65:g * 65 + 64], func=AF.Copy,
                            scale=den[:, g:g + 1]), reads=[("bank", bo), "denA"], writes=[("ObA", n)])
            for m in range(4):
                outproj_chunk(cx, xT, Ob, "ObA", m * 128, 16, (lambda o, m=m: Wout[:, m, o * 128:(o + 1) * 128]), ("Wout", 0), OTm, cx.ident)
        with P.scope():
            ws = [P.sbuf([128, 8, 384], BF16, f"wB{i}") for i in range(2)]
            QTn = P.sbuf([128, 2048], BF16, "QTb")
            KTn = P.sbuf([128, 2560], BF16, "KTb")
            Vn = P.sbuf([128, 20, 130], BF16, "Vb")
            Obn = P.sbuf([128, 16, 128], BF16, "ObB")
            tab = P.sbuf([128, 5, 2, 14, 64], BF16, "tabB")
            Esn = [P.sbuf([128, 512], F32, f"EsB{i}") for i in range(2)]
            PTn = [P.sbuf([128, 512], BF16, f"PTB{i}") for i in range(2)]
            denn = P.sbuf([128, 2], F32, "denB")
            for hp in range(4):
                w = ws[hp % 2]
                wk = [("wB", hp % 2, qi) for qi in range(3)]
                for qi in range(3):
                    c0 = 768 + qi * 512 + hp * 128
                    P.dma("pool", w[:, :, qi * 128:(qi + 1) * 128], winv[:, :, c0:c0 + 128], writes=[wk[qi]], stream=f"wB{hp % 2}")
                for v in range(5):
                    P.dma("pool", tab[:, v, :, :, :], D["na_tab"][v, :, 2 * hp:2 * hp + 2, :, :], writes=[("tabB", v)], stream="tabB")
                P.add("act", lambda E: E.activation(out=tab[:, :, :, :, :].rearrange("p v h s q -> p (v h s q)"),
                                                    in_=tab[:, :, :, :, :].rearrange("p v h s q -> p (v h s q)"), func=AF.Exp),
                      reads=[("tabB", v) for v in range(5)], writes=["tabBx"])
                attn_in_proj(cx, hT, w, wk, [(0, (lambda b0, n: QTn[:, b0:b0 + n]))], 128, 256, 2, QTn, KTn, Vn, kval,
                             gqk[:, 2:3], gqk[:, 3:4], scr, "B")
                for t in range(16):
                    u0, u1 = (0, 5) if t == 0 else ((-1, 4) if t == 15 else (0, 4))
                    var = {0: 1, 1: 2, 14: 3, 15: 4}.get(t, 0)
                    nU = u1 - u0 + 1
                    bo = cx.bank()
                    bko = cx.banks[bo]
                    for hh in range(2):
                        lo, hi = hh * 64, (hh + 1) * 64
                        sb = [cx.bank(), cx.bank()]
                        for ui in range(nU):
                            j = t + u0 + ui
                            bi = sb[ui // 4]
                            bk = cx.banks[bi]
                            P.add("pe", lambda E, ui=ui, j=j, bk=bk, lo=lo, hi=hi, t=t: E.matmul(
                                bk[:, (ui % 4) * 128:(ui % 4 + 1) * 128], lhsT=KTn[lo:hi, j * 128:(j + 1) * 128],
                                rhs=QTn[lo:hi, t * 128:(t + 1) * 128], start=True, stop=True),
                                reads=[("BKT", j // 4), ("BQT", 0, t // 4)], writes=[("bank", bi)])
                        for part in range(2):
                            cnt = min(4, nU - part * 4)
                            if cnt <= 0:
                                continue
                            bi = sb[part]
                            bk = cx.banks[bi]
                            s0 = 2 * (u0 + part * 4 + 1)
                            P.add("act", lambda E, part=part, cnt=cnt, bk=bk: E.activation(
                                out=Esn[part][:, 0:cnt * 128], in_=bk[:, 0:cnt * 128], func=AF.Exp, scale=0.125),
                                reads=[("bank", bi)], writes=[("EsB", part)])
                            eng = "pool" if part == 0 else "dve"
                            P.add(eng, lambda E, part=part, cnt=cnt, s0=s0, hh=hh, var=var: E.tensor_tensor(
                                out=PTn[part][:, 0:cnt * 128], in0=Esn[part][:, 0:cnt * 128],
                                in1=tab[:, var, hh, s0:s0 + 2 * cnt, :].rearrange("p s q -> p (s q)"), op=ALU.mult),
                                reads=[("EsB", part), "tabBx"], writes=[("PTB", part)])
                        for ui in range(nU):
                            j = t + u0 + ui
                            P.add("pe", lambda E, ui=ui, j=j, hh=hh, bko=bko: E.matmul(
                                bko[:, hh * 65:(hh + 1) * 65], lhsT=PTn[ui // 4][:, (ui % 4) * 128:(ui % 4 + 1) * 128],
                                rhs=Vn[:, j, hh * 65:(hh + 1) * 65], start=(ui == 0), stop=(ui == nU - 1)),
                                reads=[("PTB", ui // 4), ("BV", j)] + [("BVones", h) for h in range(2)], writes=[("bank", bo)])
                    bov = bko[:, 0:130].rearrange("p (g e) -> p g e", e=65)
                    P.add("dve", lambda E, bov=bov: E.reciprocal(out=denn[:, :], in_=bov[:, :, 64]), reads=[("bank", bo)], writes=["denB"])
                    for hh in range(2):
                        P.add("act", lambda E, hh=hh, t=t, bko=bko: E.activation(
                            out=Obn[:, t, hh * 64:(hh + 1) * 64], in_=bko[:, hh * 65:hh * 65 + 64], func=AF.Copy, scale=denn[:, hh:hh + 1]),
                            reads=[("bank", bo), "denB"], writes=[("ObB", t)])
                m = 4 + hp
                outproj_chunk(cx, xT, Obn, "ObB", 0, 16, (lambda o, m=m: Wout[:, m, o * 128:(o + 1) * 128]), ("Wout", 1), OTm, cx.ident)


GN_EPS = 1e-5
KSCALE = 128.0 ** -0.5


def ret_tables(cx, D, T):
    P = cx.P
    lg, u, tt = T["lg"], T["u"], T["tt"]
    P.dma("sp", lg[:], D["dec"], writes=["lg"], stream="c0")
    P.dma("sp", T["cols"][:], D["rcols"], writes=["rcols"], stream="c0")
    P.dma("sp", T["Pf"][:], D["Pf"], writes=["Pf"], stream="c0")
    P.dma("sp", T["Pb"][:], D["Pb"], writes=["Pb"], stream="c0")
    P.add("act", lambda E: E.activation(out=u[:], in_=lg[:], func=AF.Exp, scale=-1.0), reads=["lg"], writes=["u"])
    coef = [(-0.2, 0.25), (-1.0, 1.0 / 3.0), (-1.0, 0.5), (-1.0, 1.0)]
    P.add("dve", lambda E: E.tensor_scalar(out=tt[:], in0=u[:], scalar1=-0.2, scalar2=0.25, op0=ALU.mult, op1=ALU.add),
          reads=["u"], writes=["tt"])
    for (a, b) in coef[1:]:
        P.add("dve", lambda E: E.tensor_tensor(out=tt[:], in0=tt[:], in1=u[:], op=ALU.mult), reads=["tt", "u"], writes=["tt"])
        P.add("dve", lambda E, a=a, b=b: E.tensor_scalar(out=tt[:], in0=tt[:], scalar1=a, scalar2=b, op0=ALU.mult, op1=ALU.add),
              reads=["tt"], writes=["tt"])
    P.add("dve", lambda E: E.tensor_tensor(out=tt[:], in0=tt[:], in1=u[:], op=ALU.mult), reads=["tt", "u"], writes=["tt"])
    P.add("dve", lambda E: E.tensor_scalar(out=lg[:], in0=tt[:], scalar1=-1.0, scalar2=None, op0=ALU.mult), reads=["tt"], writes=["lg"])
    for name, ci, half in [("CF", 0, 0), ("CB", 1, 1), ("ZF", 2, 0), ("ZB", 3, 1)]:
        t = T[name]
        P.add("dve", lambda E, t=t, ci=ci, half=half: E.tensor_scalar(out=t[:], in0=lg[:, half * 8:(half + 1) * 8],
                                                                      scalar1=T["cols"][:, ci:ci + 1], scalar2=None, op0=ALU.mult),
              reads=["lg", "rcols"], writes=[name])
        P.add("act", lambda E, t=t: E.activation(out=t[:], in_=t[:], func=AF.Exp), reads=[name], writes=[name])
    P.add("act", lambda E: E.activation(out=T["cd"][:], in_=lg[:], func=AF.Exp, scale=128.0), reads=["lg"], writes=["cd"])
    if "DmT" in T:
        for h in range(8):
            P.add("dve", lambda E, h=h: E.tensor_scalar(out=T["dtmp"][:], in0=T["Pf"][:], scalar1=lg[:, h:h + 1], scalar2=None, op0=ALU.mult),
                  reads=["lg", "Pf"], writes=["dtmp"])
            P.add("dve", lambda E, h=h: E.scalar_tensor_tensor(out=T["dtmp"][:], in0=T["Pb"][:], scalar=lg[:, 8 + h:9 + h], in1=T["dtmp"][:],
                                                               op0=ALU.mult, op1=ALU.add), reads=["lg", "Pb", "dtmp"], writes=["dtmp"])
            P.add("act", lambda E, h=h: E.activation(out=T["DmT"][:, h, :], in_=T["dtmp"][:], func=AF.Exp), reads=["dtmp"], writes=[("DmT", h)])


def ret_alloc_tables(P, full):
    T = dict(lg=P.sbuf([128, 16], F32, "lg"), u=P.sbuf([128, 16], F32, "u"), tt=P.sbuf([128, 16], F32, "tt"),
             cols=P.sbuf([128, 4], F32, "rcols"), Pf=P.sbuf([128, 128], F32, "Pf"), Pb=P.sbuf([128, 128], F32, "Pb"),
             CF=P.sbuf([128, 8], F32, "CF"), CB=P.sbuf([128, 8], F32, "CB"), ZF=P.sbuf([128, 8], F32, "ZF"),
             ZB=P.sbuf([128, 8], F32, "ZB"), cd=P.sbuf([128, 16], F32, "cd"))
    if full:
        T["DmT"] = P.sbuf([128, 8, 128], BF16, "DmT")
        T["dtmp"] = P.sbuf([128, 128], F32, "dtmp")
    return T


def ret_kv(cx, hT, w, wkeys, kcol, vcol, KT, V, Kzf, Kzb, T, h):
    P = cx.P
    flip = [0]

    def kcons(bk, bi, b0, n):
        P.add("act", lambda E: E.activation(out=KT[:, b0:b0 + n], in_=bk[:, 0:n], func=AF.Copy, scale=KSCALE),
              reads=[("bank", bi)], writes=[("KT", b0 // 512)])
    proj_feat(cx, hT, 0, 2048, w, wkeys, kcol, kcons)

    def vcons(bk, bi, j):
        flip[0] ^= 1
        if flip[0]:
            P.add("act", lambda E: E.copy(out=V[:, j, :], in_=bk[:, 0:256]), reads=[("bank", bi)], writes=[("V", j)])
        else:
            P.add("dve", lambda E: E.tensor_copy(out=V[:, j, :], in_=bk[:, 0:256]), reads=[("bank", bi)], writes=[("V", j)])
    proj_tok(cx, hT, 16, w, wkeys, vcol, 256, vcons)
    for n0 in range(0, 16, 4):
        bi = cx.bank()
        bkb = cx.banks[bi][:, :].bitcast(BF16)
        for n in range(n0, n0 + 4):
            P.add("pe", lambda E, n=n, n0=n0, bkb=bkb: E.transpose(out=bkb[:, (n - n0) * 128:(n - n0 + 1) * 128],
                                                                   in_=KT[:, n * 128:(n + 1) * 128], identity=cx.ident[:]),
                  reads=[("KT", n // 4), "ident"], writes=[("bank", bi)])
        P.add("act", lambda E, n0=n0, bkb=bkb: E.activation(out=Kzf[:, n0:n0 + 4, :].rearrange("p n d -> p (n d)"), in_=bkb[:, 0:512],
                                                            func=AF.Copy, scale=T["ZF"][:, h:h + 1]),
              reads=[("bank", bi), "ZF"], writes=[("Kzf", n0 // 4)])
        P.add("dve", lambda E, n0=n0, bkb=bkb: E.tensor_scalar(out=Kzb[:, n0:n0 + 4, :].rearrange("p n d -> p (n d)"), in0=bkb[:, 0:512],
                                                               scalar1=T["ZB"][:, h:h + 1], scalar2=None, op0=ALU.mult),
              reads=[("bank", bi), "ZB", ("Kzf", n0 // 4)], writes=[("Kzb", n0 // 4)])


def ret_state_step(cx, Kz, kzkey, V, n, R, rkey, cdcol):
    P = cx.P
    bi = cx.bank()
    bk = cx.banks[bi]
    P.add("pe", lambda E: E.matmul(bk[:, 0:256], lhsT=Kz[:, n, :], rhs=V[:, n, :], start=True, stop=True),
          reads=[(kzkey, n // 4), ("V", n)], writes=[("bank", bi)])
    P.add("dve", lambda E: E.scalar_tensor_tensor(out=R[:], in0=R[:], scalar=cdcol, in1=bk[:, 0:256], op0=ALU.mult, op1=ALU.add),
          reads=[("bank", bi), rkey, "cd"], writes=[rkey])


def ret_pass1(cx, xT, D, Tout):
    P = cx.P
    with P.scope():
        hT = P.sbuf([128, 8, 2048], BF16, "hT1")
        gr = P.sbuf([128, 8], F32, "gr")
        P.dma("sp", gr[:], D["g_ret"].rearrange("(c p) -> p c", p=128), writes=["gains"], stream="c0", allow_slow_non_contiguous=True)
        with P.scope():
            rscr = dict(sq=P.sbuf([128, 8, 256], F32, "sq"), acc=P.sbuf([128, 512], F32, "acc"), rstd=P.sbuf([128, 512], F32, "rstd"))
            for tb in range(4):
                rms_seg(cx, xT, ("xT", tb), tb * 512, 512, hT, tb * 512, hkeys(tb * 512, tb * 512 + 512), lambda c: gr[:, c:c + 1], rscr)
        T = ret_alloc_tables(P, False)
        ret_tables(cx, D, T)
        ws = [P.sbuf([128, 8, 384], BF16, f"w1p{i}") for i in range(2)]
        KT = P.sbuf([128, 2048], BF16, "KT1")
        V = P.sbuf([128, 16, 256], BF16, "V1")
        Kzf = P.sbuf([128, 16, 128], BF16, "Kzf")
        Kzb = P.sbuf([128, 16, 128], BF16, "Kzb")
        Rf = P.sbuf([128, 256], F32, "Rf")
        Rb = P.sbuf([128, 256], F32, "Rb")
        winv = D["win_o"].rearrange("(c p) f -> p c f", p=128)
        for h in range(8):
            w = ws[h % 2]
            wk = [("w1p", h % 2, 0), ("w1p", h % 2, 1)]
            P.dma("pool", w[:, :, 0:128], winv[:, :, 1024 + h * 128:1024 + (h + 1) * 128], writes=[wk[0]], stream=f"w1p{h % 2}")
            P.dma("pool", w[:, :, 128:384], winv[:, :, 2048 + h * 256:2048 + (h + 1) * 256], writes=[wk[1]], stream=f"w1p{h % 2}")
            ret_kv(cx, hT, w, wk, 0, 128, KT, V, Kzf, Kzb, T, h)
            P.add("pool", lambda E: E.memset(Rf[:], 0.0), writes=["Rf"])
            P.add("pool", lambda E: E.memset(Rb[:], 0.0), writes=["Rb"])
            for n in range(16):
                ret_state_step(cx, Kzf, "Kzf", V, n, Rf, "Rf", T["cd"][:, h:h + 1])
            for n in range(15, -1, -1):
                ret_state_step(cx, Kzb, "Kzb", V, n, Rb, "Rb", T["cd"][:, 8 + h:9 + h])
            P.dma("sp", Tout[:, 0, h, :], Rf[:], reads=["Rf"], stream="tout")
            P.dma("sp", Tout[:, 1, h, :], Rb[:], reads=["Rb"], stream="tout")


def ret_main(cx, xT, D):
    P = cx.P
    with P.scope():
        hT = P.sbuf([128, 8, 2048], BF16, "hT1")
        gr = P.sbuf([128, 8], F32, "gr")
        P.dma("sp", gr[:], D["g_ret"].rearrange("(c p) -> p c", p=128), writes=["gains"], stream="c0", allow_slow_non_contiguous=True)
        with P.scope():
            rscr = dict(sq=P.sbuf([128, 8, 256], F32, "sq"), acc=P.sbuf([128, 512], F32, "acc"), rstd=P.sbuf([128, 512], F32, "rstd"))
            for tb in range(4):
                rms_seg(cx, xT, ("xT", tb), tb * 512, 512, hT, tb * 512, hkeys(tb * 512, tb * 512 + 512), lambda c: gr[:, c:c + 1], rscr)
        T = ret_alloc_tables(P, True)
        ret_tables(cx, D, T)
        lg = T["lg"]
        rcw = P.sbuf([128, 32], F32, "rcw")
        wf = P.sbuf([128, 8, 8], F32, "wf")
        wb = P.sbuf([128, 8, 8], F32, "wb")
        P.dma("sp", rcw[:], D["rcw"], writes=["rcw"], stream="c0")
        for (wt, nm, half, eo, mo) in [(wf, "wf", 0, 0, 8), (wb, "wb", 1, 16, 24)]:
            for cp in range(8):
                P.add("dve", lambda E, wt=wt, cp=cp, half=half, eo=eo: E.tensor_scalar(
                    out=wt[:, cp, :], in0=lg[:, half * 8:(half + 1) * 8], scalar1=rcw[:, eo + cp:eo + cp + 1], scalar2=None, op0=ALU.mult),
                    reads=["lg", "rcw"], writes=[nm])
            P.add("act", lambda E, wt=wt: E.activation(out=wt[:, :, :].rearrange("p c h -> p (c h)"), in_=wt[:, :, :].rearrange("p c h -> p (c h)"),
                                                       func=AF.Exp), reads=[nm], writes=[nm])
            for cp in range(8):
                P.add("dve", lambda E, wt=wt, cp=cp, mo=mo: E.tensor_scalar(
                    out=wt[:, cp, :], in0=wt[:, cp, :], scalar1=rcw[:, mo + cp:mo + cp + 1], scalar2=None, op0=ALU.mult),
                    reads=[nm, "rcw"], writes=[nm])
        ws = [P.sbuf([128, 8, 768], BF16, f"w1m{i}") for i in range(1)]
        QT = P.sbuf([128, 2048], BF16, "QT1")
        KT = P.sbuf([128, 2048], BF16, "KT1")
        V = P.sbuf([128, 16, 256], BF16, "V1")
        G = P.sbuf([128, 16, 256], BF16, "G1")
        Kzf = P.sbuf([128, 16, 128], BF16, "Kzf")
        Kzb = P.sbuf([128, 16, 128], BF16, "Kzb")
        Rbs = P.sbuf([128, 16, 256], BF16, "Rbs")
        Z = P.sbuf([128, 16, 256], BF16, "Z1")
        Tal = P.sbuf([128, 8, 2, 256], F32, "Tal")
        Rf = P.sbuf([128, 256], F32, "Rf")
        Rb = P.sbuf([128, 256], F32, "Rb")
        Rfb = P.sbuf([128, 256], BF16, "Rfb")
        AT = P.sbuf([128, 128], BF16, "AT")
        ysq = P.sbuf([128, 256], F32, "ysq")
        yc = P.sbuf([128, 256], F32, "yc")
        st = P.sbuf([128, 8], F32, "gnst")
        GN = P.sbuf([128, 256], F32, "GNh")
        Wo = P.sbuf([128, 2, 1024], BF16, "Wo1")
        OTm = P.sbuf([128, 2048], BF16, "OTm1")
        epsg = P.sbuf([128, 1], F32, "epsg")
        P.add("pool", lambda E: E.memset(epsg[:], GN_EPS), writes=["epsg"])
        winv = D["win_o"].rearrange("(c p) f -> p c f", p=128)
        wov = D["wout_o"].rearrange("(m p) o -> p m o", p=128)
        for h in range(8):
            w = ws[0]
            wk = [("w1m", i) for i in range(4)]
            for i, (c0, c1, d0) in enumerate([(h * 128, (h + 1) * 128, 0), (1024 + h * 128, 1024 + (h + 1) * 128, 128),
                                               (2048 + h * 256, 2048 + (h + 1) * 256, 256), (4096 + h * 256, 4096 + (h + 1) * 256, 512)]):
                P.dma("pool", w[:, :, d0:d0 + (c1 - c0)], winv[:, :, c0:c1], writes=[wk[i]], stream="w1m")
            P.dma("pool", Wo[:, :, :], wov[:, 2 * h:2 * h + 2, :], writes=["Wo1"], stream="wo1")
            P.dma("sp", GN[:], D["gn_rep"][:, h * 256:(h + 1) * 256], writes=["GNh"], stream="gn")
            for di in range(2):
                P.dma("sp", Tal[:, :, di, :], D["Tall"][:, :, di, h, :].rearrange("c p v -> p c v"), writes=[("Tal", di)], stream="tal")
            for (R, rk, wt, nm, di) in [(Rf, "Rf", wf, "wf", 0), (Rb, "Rb", wb, "wb", 1)]:
                P.add("dve", lambda E, R=R, wt=wt, di=di, h=h: E.tensor_scalar(out=R[:], in0=Tal[:, 0, di, :], scalar1=wt[:, 0, h:h + 1],
                                                                               scalar2=None, op0=ALU.mult),
                      reads=[("Tal", di), nm], writes=[rk])
                for cp in range(1, 8):
                    P.add("dve", lambda E, R=R, wt=wt, di=di, h=h, cp=cp: E.scalar_tensor_tensor(
                        out=R[:], in0=Tal[:, cp, di, :], scalar=wt[:, cp, h:h + 1], in1=R[:], op0=ALU.mult, op1=ALU.add),
                        reads=[("Tal", di), nm, rk], writes=[rk])
            def qcons(bk, bi, b0, n):
                P.add("act", lambda E: E.copy(out=QT[:, b0:b0 + n], in_=bk[:, 0:n]), reads=[("bank", bi)], writes=[("QT", b0 // 512)])
            proj_feat(cx, hT, 0, 2048, w, wk, 0, qcons)
            ret_kv(cx, hT, w, wk, 128, 256, KT, V, Kzf, Kzb, T, h)

            def gcons(bk, bi, j):
                P.add("act", lambda E: E.activation(out=G[:, j, :], in_=bk[:, 0:256], func=AF.Silu), reads=[("bank", bi)], writes=[("G", j)])
            proj_tok(cx, hT, 16, w, wk, 512, 256, gcons)
            for n in range(15, -1, -1):
                P.add("pool", lambda E, n=n: E.tensor_copy(out=Rbs[:, n, :], in_=Rb[:]), reads=["Rb"], writes=[("Rbs", n)])
                if n > 0:
                    ret_state_step(cx, Kzb, "Kzb", V, n, Rb, "Rb", T["cd"][:, 8 + h:9 + h])
            for n in range(16):
                P.add("pool", lambda E: E.tensor_copy(out=Rfb[:], in_=Rf[:]), reads=["Rf"], writes=["Rfb"])
                ba, bf_, bb = cx.bank(), cx.bank(), cx.bank()
                bka, bkf, bkb_ = cx.banks[ba], cx.banks[bf_], cx.banks[bb]
                P.add("pe", lambda E, n=n, bka=bka: E.matmul(bka[:, 0:128], lhsT=KT[:, n * 128:(n + 1) * 128], rhs=QT[:, n * 128:(n + 1) * 128],
                                                              start=True, stop=True),
                      reads=[("KT", n // 4), ("QT", n // 4)], writes=[("bank", ba)])
                P.add("dve", lambda E, bka=bka, h=h: E.tensor_tensor(out=AT[:], in0=bka[:, 0:128], in1=T["DmT"][:, h, :], op=ALU.mult),
                      reads=[("bank", ba), ("DmT", h)], writes=["AT"])
                P.add("pe", lambda E, n=n, bkf=bkf: E.matmul(bkf[:, 0:256], lhsT=QT[:, n * 128:(n + 1) * 128], rhs=Rfb[:], start=True, stop=True),
                      reads=[("QT", n // 4), "Rfb"], writes=[("bank", bf_)])
                P.add("pe", lambda E, n=n, bkb_=bkb_: E.matmul(bkb_[:, 0:256], lhsT=QT[:, n * 128:(n + 1) * 128], rhs=Rbs[:, n, :], start=True, stop=True),
                      reads=[("QT", n // 4), ("Rbs", n)], writes=[("bank", bb)])
                P.add("pe", lambda E, n=n, bka=bka: E.matmul(bka[:, 256:512], lhsT=AT[:], rhs=V[:, n, :], start=True, stop=True),
                      reads=["AT", ("V", n)], writes=[("bank", ba)])
                P.add("act", lambda E, bka=bka: E.copy(out=yc[:], in_=bka[:, 256:512]), reads=[("bank", ba)], writes=["yc"])
                P.add("dve", lambda E, bkf=bkf, h=h: E.scalar_tensor_tensor(out=yc[:], in0=bkf[:, 0:256], scalar=T["CF"][:, h:h + 1],
                                                                           in1=yc[:], op0=ALU.mult, op1=ALU.add),
                      reads=[("bank", bf_), "CF", "yc"], writes=["yc"])
                P.add("dve", lambda E, bkb_=bkb_, h=h: E.scalar_tensor_tensor(out=yc[:], in0=bkb_[:, 0:256], scalar=T["CB"][:, h:h + 1],
                                                                              in1=yc[:], op0=ALU.mult, op1=ALU.add),
                      reads=[("bank", bb), "CB", "yc"], writes=["yc"])
                P.add("dve", lambda E: E.reduce_sum(out=st[:, 0:1], in_=yc[:], axis=AX.X), reads=["yc"], writes=["st0"])
                P.add("act", lambda E: E.activation(out=ysq[:], in_=yc[:], func=AF.Square), reads=["yc"], writes=["ysq"])
                P.add("dve", lambda E: E.reduce_sum(out=st[:, 1:2], in_=ysq[:], axis=AX.X), reads=["ysq"], writes=["st1"])
                P.add("dve", lambda E: E.tensor_scalar(out=st[:, 2:3], in0=st[:, 0:1], scalar1=1.0 / 256.0, scalar2=None, op0=ALU.mult),
                      reads=["st0"], writes=["st2"])
                P.add("dve", lambda E: E.tensor_tensor(out=st[:, 3:4], in0=st[:, 2:3], in1=st[:, 2:3], op=ALU.mult), reads=["st2"], writes=["st3"])
                P.add("dve", lambda E: E.scalar_tensor_tensor(out=st[:, 4:5], in0=st[:, 1:2], scalar=1.0 / 256.0, in1=st[:, 3:4],
                                                              op0=ALU.mult, op1=ALU.subtract), reads=["st1", "st3"], writes=["st4"])
                P.add("act", lambda E: E.activation(out=st[:, 5:6], in_=st[:, 4:5], func=AF.Sqrt, bias=epsg[:, 0:1], scale=1.0),
                      reads=["st4", "epsg"], writes=["st5"])
                P.add("dve", lambda E: E.reciprocal(out=st[:, 6:7], in_=st[:, 5:6]), reads=["st5"], writes=["st6"])
                P.add("dve", lambda E: E.tensor_scalar(out=yc[:], in0=yc[:], scalar1=st[:, 2:3], scalar2=st[:, 6:7], op0=ALU.subtract, op1=ALU.mult),
                      reads=["yc", "st2", "st6"], writes=["yc"])
                P.add("pool", lambda E: E.tensor_tensor(out=yc[:], in0=yc[:], in1=GN[:], op=ALU.mult), reads=["yc", "GNh"], writes=["yc"])
                P.add("pool", lambda E, n=n: E.tensor_tensor(out=Z[:, n, :], in0=yc[:], in1=G[:, n, :], op=ALU.mult),
                      reads=["yc", ("G", n)], writes=[("Z", n)])
                if n < 15:
                    ret_state_step(cx, Kzf, "Kzf", V, n, Rf, "Rf", T["cd"][:, h:h + 1])
            for mm in range(2):
                outproj_chunk(cx, xT, Z, "Z", mm * 128, 16, (lambda o, mm=mm: Wo[:, mm, o * 128:(o + 1) * 128]), "Wo1", OTm, cx.ident)


SHAPES_A = dict(xT=[1024, 2048], xh=[1024, 512], kval=[128, 20], win=[1024, 2304], gqk=[128, 4], sink=[128, 8],
                g_attn=[1024], gqa_tab=[128, 3072], na_tab=[5, 128, 8, 14, 64], wout=[1024, 1024], ident=[128, 128],
                g_mlp=[1024], w1=[1024, 4096], w2=[4096, 1024])


def mlp_scope(cx, xT, g_dram, w1, w2):
    P = cx.P
    with P.scope():
        gm = P.sbuf([128, 8], F32, "gm")
        P.dma("sp", gm[:], g_dram.rearrange("(c p) -> p c", p=128), writes=["gains"], stream="c0", allow_slow_non_contiguous=True)
        scr = dict(
            sq=P.sbuf([128, 8, 256], F32, name="sq"), acc=P.sbuf([128, 512], F32, name="acc"),
            rstd=P.sbuf([128, 512], F32, name="rstd"),
            W2=P.sbuf([128, 32, 1024], BF16, name="W2"),
            W1b=[P.sbuf([128, 8, 512], BF16, name=f"W1b{i}") for i in range(2)],
            aT=P.sbuf([128, 32, 512], BF16, name="aT"), hT=P.sbuf([128, 8, 512], BF16, name="hTm"),
            rl=[P.sbuf([128, 512], F32, name=f"rl{i}") for i in range(2)],
        )
        mlp_T(cx, 0, xT, w1, w2, lambda c: gm[:, c:c + 1], scr)


def build_A(do_mixer=True, do_mlp=True):
    nc = bass.Bass("TRN2", target_bir_lowering=False)
    D = {k: nc.dram_tensor(k, s, F32, kind="ExternalInput").ap() for k, s in SHAPES_A.items()}
    yout = nc.dram_tensor("yT", [1024, 2048], F32, kind="ExternalOutput").ap()
    P = Prog(nc)
    cx = Ctx(P, 2048, ident_dram=D["ident"])
    xT = P.sbuf([128, 8, 2048], F32, name="xT_sb")
    xv = D["xT"].rearrange("(c p) t -> p c t", p=128)
    for tb in range(4):
        P.dma("sp", xT[:, :, tb * 512:(tb + 1) * 512], xv[:, :, tb * 512:(tb + 1) * 512], writes=[("xT", tb)], stream="xin")
    if do_mixer:
        even_mixer(cx, xT, D)
    if do_mlp:
        mlp_scope(cx, xT, D["g_mlp"], D["w1"], D["w2"])
    yv = yout.rearrange("(c p) t -> p c t", p=128)
    for tb in range(4):
        P.dma("sp", yv[:, :, tb * 512:(tb + 1) * 512], xT[:, :, tb * 512:(tb + 1) * 512], reads=[("xT", tb)], stream="yout")
    P.emit(final_waits=["yout"])
    print("ops:", {e: len(P.ops[e]) for e in P.ENGS}, "signals:", P.sig_counts, flush=True)
    return nc

SHAPES_B = dict(xT=[1024, 2048], g_ret=[1024], win_o=[1024, 6144], dec=[128, 16], rcols=[128, 4], Pf=[128, 128], Pb=[128, 128],
                ident=[128, 128])
SHAPES_C = dict(SHAPES_B, wout_o=[2048, 1024], gn_rep=[128, 2048], Tall=[8, 128, 2, 8, 256], rcw=[128, 32],
                g_mlp=[1024], w1=[1024, 4096], w2=[4096, 1024])


def load_x(P, xT, D):
    xv = D["xT"].rearrange("(c p) t -> p c t", p=128)
    for tb in range(4):
        P.dma("sp", xT[:, :, tb * 512:(tb + 1) * 512], xv[:, :, tb * 512:(tb + 1) * 512], writes=[("xT", tb)], stream="xin")


def build_B():
    nc = bass.Bass("TRN2", target_bir_lowering=False)
    D = {k: nc.dram_tensor(k, s, F32, kind="ExternalInput").ap() for k, s in SHAPES_B.items()}
    tout = nc.dram_tensor("Tout", [128, 2, 8, 256], F32, kind="ExternalOutput").ap()
    P = Prog(nc)
    cx = Ctx(P, 2048, ident_dram=D["ident"])
    xT = P.sbuf([128, 8, 2048], F32, name="xT_sb")
    load_x(P, xT, D)
    ret_pass1(cx, xT, D, tout)
    P.emit(final_waits=["tout"])
    print("B ops:", {e: len(P.ops[e]) for e in P.ENGS}, "signals:", P.sig_counts, flush=True)
    return nc


def build_C(do_mlp=True):
    nc = bass.Bass("TRN2", target_bir_lowering=False)
    D = {k: nc.dram_tensor(k, s, F32, kind="ExternalInput").ap() for k, s in SHAPES_C.items()}
    yout = nc.dram_tensor("yT", [1024, 2048], F32, kind="ExternalOutput").ap()
    P = Prog(nc)
    cx = Ctx(P, 2048, ident_dram=D["ident"])
    xT = P.sbuf([128, 8, 2048], F32, name="xT_sb")
    load_x(P, xT, D)
    ret_main(cx, xT, D)
    if do_mlp:
        mlp_scope(cx, xT, D["g_mlp"], D["w1"], D["w2"])
    yv = yout.rearrange("(c p) t -> p c t", p=128)
    for tb in range(4):
        P.dma("sp", yv[:, :, tb * 512:(tb + 1) * 512], xT[:, :, tb * 512:(tb + 1) * 512], reads=[("xT", tb)], stream="yout")
    P.emit(final_waits=["yout"])
    print("C ops:", {e: len(P.ops[e]) for e in P.ENGS}, "signals:", P.sig_counts, flush=True)
    return nc


import numpy as np

NCORE = 8
TPC = 2048


def alibi_slopes(n):
    return (2.0 ** (-(8.0 / n) * (np.arange(n, dtype=np.float32) + 1.0))).astype(np.float32)


def gqa_table():
    k = np.arange(128)[:, None, None, None]
    rp = np.arange(3)[None, :, None, None]
    h = np.arange(8)[None, None, :, None]
    q = np.arange(128)[None, None, None, :]
    rel = (rp - 1) * 128 + k - q
    sl = alibi_slopes(8)[h]
    a = np.where(np.abs(rel) <= 128, -sl * np.abs(rel).astype(np.float32), np.float32(-30000.0)).astype(np.float32)
    a = np.broadcast_to(a, (128, 3, 8, 128))
    return np.ascontiguousarray(a).reshape(128, 3072)


def na_table(rpb, core):
    out = np.empty((5, 128, 8, 14, 64), np.float32)
    p = np.arange(128)
    a = (p // 64)[:, None, None]
    kc = (p % 64)[:, None, None]
    sl = np.arange(14)[None, :, None]
    qc = np.arange(64)[None, None, :]
    u = sl // 2 - 1
    b = sl % 2
    dr = 2 * u - 4 + a - b
    cstart = np.clip(qc - 8, 0, 64 - 16)
    colv = (kc >= cstart) & (kc < cstart + 16)
    cidx = np.clip(kc - qc + 15, 0, 30)
    tmap = {0: None, 1: 0, 2: 1, 3: 14, 4: 15}
    for v in range(5):
        t = tmap[v]
        if t is None:
            r = 100 + b
        else:
            r = core * 32 + 2 * t + b
        rstart = np.clip(r - 4, 0, 256 - 8)
        kr = r + dr
        rowv = (kr >= rstart) & (kr < rstart + 8) & (np.abs(dr) <= 7)
        ridx = np.clip(dr + 7, 0, 14)
        valid = np.broadcast_to(rowv & colv, (128, 14, 64))
        ri = np.broadcast_to(ridx, (128, 14, 64))
        ci = np.broadcast_to(cidx, (128, 14, 64))
        for h in range(8):
            g = rpb[h][ri, ci]
            out[v, :, h] = np.where(valid, g, np.float32(-30000.0))
    return out


def prep_layer0(inp, core):
    x = inp["x"][0]
    s0 = core * TPC
    d = {}
    d["xT"] = np.ascontiguousarray(x[s0:s0 + TPC].T)
    left = x[s0 - 256:s0] if core > 0 else np.zeros((256, 1024), np.float32)
    right = x[s0 + TPC:s0 + TPC + 256] if core < NCORE - 1 else np.zeros((256, 1024), np.float32)
    d["xh"] = np.ascontiguousarray(np.concatenate([left, right], 0).T)
    kv = np.ones((128, 20), np.float32)
    if core == 0:
        kv[:, 0:2] = 0
    if core == NCORE - 1:
        kv[:, 18:20] = 0
    d["kval"] = kv
    w = inp["w_in_e"][0]
    qa = w[:, 0:512].reshape(1024, 8, 64)
    perm = [0, 4, 1, 5, 2, 6, 3, 7]
    qa = qa[:, perm, :].reshape(1024, 512)
    d["win"] = np.ascontiguousarray(np.concatenate([qa, w[:, 512:]], 1))
    g = np.stack([np.tile(inp["q_norm_a"][0], 2), np.tile(inp["k_norm_a"][0], 2),
                  np.tile(inp["q_norm_b"][0], 2), np.tile(inp["k_norm_b"][0], 2)], 1).astype(np.float32)
    d["gqk"] = np.ascontiguousarray(g)
    d["sink"] = np.ascontiguousarray(np.broadcast_to(inp["sink_a"][0][None, :], (128, 8)).astype(np.float32))
    d["g_attn"] = np.ascontiguousarray(inp["attn_norm_e"][0])
    d["gqa_tab"] = gqa_table()
    d["na_tab"] = na_table(inp["rpb_b"][0], core)
    d["wout"] = np.ascontiguousarray(inp["w_out_e"][0])
    d["ident"] = np.eye(128, dtype=np.float32)
    d["g_mlp"] = np.ascontiguousarray(inp["mlp_norm"][0])
    d["w1"] = np.ascontiguousarray(inp["w_mlp_in"][0])
    d["w2"] = np.ascontiguousarray(inp["w_mlp_out"][0])
    return d


def prep_layer1_common(inp, x1T):
    d = {}
    d["xT"] = np.ascontiguousarray(x1T)
    d["g_ret"] = np.ascontiguousarray(inp["ret_norm_o"][0])
    d["win_o"] = np.ascontiguousarray(inp["w_in_o"][0])
    dec = np.concatenate([inp["decay_fwd_o"][0], inp["decay_bwd_o"][0]]).astype(np.float32)
    d["dec"] = np.ascontiguousarray(np.broadcast_to(dec[None, :], (128, 16)))
    i = np.arange(128, dtype=np.float32)
    d["rcols"] = np.ascontiguousarray(np.stack([i + 1, 128 - i, 127 - i, i], 1).astype(np.float32))
    jj = i[:, None]
    ii = i[None, :]
    d["Pf"] = np.ascontiguousarray(np.maximum(ii - jj, 0).astype(np.float32))
    d["Pb"] = np.ascontiguousarray(np.maximum(jj - ii, 0).astype(np.float32))
    d["ident"] = np.eye(128, dtype=np.float32)
    return d


def prep_layer1_C(inp, x1T, Tall, core):
    d = prep_layer1_common(inp, x1T)
    d["wout_o"] = np.ascontiguousarray(inp["w_out_o"][0])
    d["gn_rep"] = np.ascontiguousarray(np.broadcast_to(inp["ret_gn_o"][0][None, :], (128, 2048)).astype(np.float32))
    d["Tall"] = Tall
    cp = np.arange(8)
    ef = np.where(cp < core, 2048.0 * (core - cp - 1), 0.0)
    mf = (cp < core).astype(np.float32)
    eb = np.where(cp > core, 2048.0 * (cp - core - 1), 0.0)
    mb = (cp > core).astype(np.float32)
    row = np.concatenate([ef, mf, eb, mb]).astype(np.float32)
    d["rcw"] = np.ascontiguousarray(np.broadcast_to(row[None, :], (128, 32)))
    d["g_mlp"] = np.ascontiguousarray(inp["mlp_norm"][1])
    d["w1"] = np.ascontiguousarray(inp["w_mlp_in"][1])
    d["w2"] = np.ascontiguousarray(inp["w_mlp_out"][1])
    return d

from concourse.bass_utils import run_bass_kernel_spmd

_CACHE = {}


def _prog(name, fn):
    if name not in _CACHE:
        _CACHE[name] = fn()
    return _CACHE[name]


def kernel(**inp):
    inp = {k: np.asarray(v) for k, v in inp.items()}
    cores = list(range(8))
    ncA = build_A(True, True)
    resA = run_bass_kernel_spmd(ncA, [prep_layer0(inp, c) for c in cores], core_ids=cores)
    x1T = [np.asarray(resA.results[c]["yT"]) for c in cores]
    ncB = build_B()
    resB = run_bass_kernel_spmd(ncB, [prep_layer1_common(inp, x1T[c]) for c in cores], core_ids=cores)
    Tall = np.ascontiguousarray(np.stack([np.asarray(resB.results[c]["Tout"]) for c in cores], 0))
    ncC = build_C(True)
    resC = run_bass_kernel_spmd(ncC, [prep_layer1_C(inp, x1T[c], Tall, c) for c in cores], core_ids=cores)
    y = np.concatenate([np.asarray(resC.results[c]["yT"]).T for c in cores], 0)
    return np.ascontiguousarray(y[None].astype(np.float32))
```

```python
import contextlib
import numpy as np
import concourse.bass as bass
import concourse.mybir as mybir

F32 = mybir.dt.float32
BF16 = mybir.dt.bfloat16
ALU = mybir.AluOpType
AF = mybir.ActivationFunctionType
AX = mybir.AxisListType

SAME_ENGINE_SYNC = True


class Op:
    __slots__ = ("eng", "fn", "waits", "signal", "sigval", "seq", "dma", "stream", "dmaval", "dmainc", "dwaits")

    def __init__(self, eng, fn):
        self.eng = eng
        self.fn = fn
        self.waits = []
        self.dwaits = {}
        self.signal = False
        self.sigval = None
        self.seq = None
        self.dma = False
        self.stream = None
        self.dmaval = None


class Prog:
    ENGS = ("pe", "act", "dve", "pool", "sp")

    def __init__(self, nc):
        self.nc = nc
        self.ops = {e: [] for e in self.ENGS}
        self.lastw = {}
        self.readers = {}
        self.streams = {}
        self.waited = {e: {} for e in self.ENGS}
        self.stack = contextlib.ExitStack()
        self.stacks = [self.stack]
        self.n_alloc = 0
        self.pending = {e: None for e in self.ENGS}

    @contextlib.contextmanager
    def scope(self):
        st = contextlib.ExitStack()
        self.stacks.append(st)
        try:
            yield
        finally:
            self.stacks.pop()
            st.close()
            self.barrier()

    def barrier(self):
        last = []
        for e in self.ENGS:
            for o in reversed(self.ops[e]):
                if not o.dma:
                    last.append(o)
                    break
        dm = [(s, c[0]) for s, c in self.streams.items()]
        for e in self.ENGS:
            self.pending[e] = (last, dm)

    def sbuf(self, shape, dt, name=None):
        self.n_alloc += 1
        name = f"sb{self.n_alloc}_" + (name or "")
        return self.stacks[-1].enter_context(self.nc.sbuf_tensor(name, list(shape), dt))

    def psum(self, shape, dt, name=None):
        self.n_alloc += 1
        name = f"ps{self.n_alloc}_" + (name or "")
        return self.stack.enter_context(self.nc.psum_tensor(name, list(shape), dt))

    def _dep(self, op, prod):
        if prod is None or prod is op:
            return
        e = op.eng
        if prod.dma:
            k = ("dma", prod.stream)
            if self.waited[e].get(k, 0) >= prod.dmaval:
                return
            cur = self.streams[prod.stream][0]
            if op.dma and op.stream == prod.stream:
                cur -= op.dmainc
            self.waited[e][k] = cur
            op.dwaits[prod.stream] = max(op.dwaits.get(prod.stream, 0), cur)
            return
        if prod.eng == e:
            if e == "pe" or not SAME_ENGINE_SYNC:
                return
        if self.waited[e].get(prod.eng, -1) >= prod.seq:
            return
        self.waited[e][prod.eng] = prod.seq
        prod.signal = True
        op.waits.append(("op", prod))

    def add(self, eng, fn, reads=(), writes=(), dma_stream=None, dma_inc=16):
        op = Op(eng, fn)
        op.seq = len(self.ops[eng])
        if dma_stream is not None:
            op.dma = True
            op.stream = dma_stream
            c = self.streams.setdefault(dma_stream, [0])
            c[0] += dma_inc
            op.dmainc = dma_inc
            op.dmaval = c[0]
        pend = self.pending[eng]
        if pend is not None:
            self.pending[eng] = None
            for prod in pend[0]:
                if prod.dma:
                    continue
                self._dep(op, prod)
            for sname, val in pend[1]:
                k = ("dma", sname)
                if self.waited[eng].get(k, 0) < val:
                    self.waited[eng][k] = val
                    op.dwaits[sname] = max(op.dwaits.get(sname, 0), val)
        for k in reads:
            self._dep(op, self.lastw.get(k))
        for k in writes:
            self._dep(op, self.lastw.get(k))
            for r in self.readers.get(k, ()):
                self._dep(op, r)
        for k in reads:
            self.readers.setdefault(k, []).append(op)
        for k in writes:
            self.lastw[k] = op
            self.readers[k] = []
        self.ops[eng].append(op)
        return op

    def dma(self, eng, out, in_, reads=(), writes=(), stream="d0", **kw):
        return self.add(eng, lambda E: E.dma_start(out=out, in_=in_, **kw), reads, writes, dma_stream=stream)

    def emit(self, final_waits=()):
        nc = self.nc
        st = self.stack
        semE = {e: st.enter_context(nc.semaphore("s_" + e)) for e in self.ENGS}
        semD = {s: st.enter_context(nc.semaphore("d_" + s)) for s in self.streams}
        for e in self.ENGS:
            n = 0
            for op in self.ops[e]:
                if op.signal and not op.dma:
                    n += 1
                    op.sigval = n
        self.sig_counts = {e: sum(1 for o in self.ops[e] if o.signal and not o.dma) for e in self.ENGS}
        block = st.enter_context(nc.Block())

        def run(eng_name):
            def body(E):
                for op in self.ops[eng_name]:
                    for sname, val in op.dwaits.items():
                        E.wait_ge(semD[sname], val)
                    for w in op.waits:
                        E.wait_ge(semE[w[1].eng], w[1].sigval)
                    ins = op.fn(E)
                    if op.dma:
                        ins.then_inc(semD[op.stream], op.dmainc)
                    elif op.signal:
                        ins.then_inc(semE[eng_name], 1)
                if eng_name == "sp":
                    for s in final_waits:
                        E.wait_ge(semD[s], self.streams[s][0])
            return body

        block.tensor(run("pe"))
        block.scalar(run("act"))
        block.vector(run("dve"))
        block.gpsimd(run("pool"))
        block.sync(run("sp"))

    def close(self):
        self.stack.close()


RMS_EPS = 1e-6


class Ctx:
    def __init__(self, P, NT, ident_dram=None):
        self.P = P
        self.NT = NT
        nc = P.nc
        self.banks = [P.psum([128, 512], F32, name=f"bank{i}") for i in range(8)]
        self.bank_rr = 0
        self.ones_f = P.sbuf([128, 128], F32, name="ones_f")
        P.add("pool", lambda E: E.memset(self.ones_f[:], 1.0 / 1024.0), writes=["ones_f"])
        self.eps_t = P.sbuf([128, 1], F32, name="eps_t")
        self.blockones = P.sbuf([128, 128], F32, name="blockones")
        P.add("pool", lambda E: E.memset(self.blockones[:], 0.0), writes=["blockones"])
        P.add("pool", lambda E: E.memset(self.blockones[0:64, 0:64], 1.0 / 64.0), writes=["blockones"])
        P.add("pool", lambda E: E.memset(self.blockones[64:128, 64:128], 1.0 / 64.0), writes=["blockones"])
        self.ident = P.sbuf([128, 128], BF16, name="ident")
        if ident_dram is not None:
            P.dma("pool", self.ident[:], ident_dram, writes=["ident"], stream="c0p")
        P.add("pool", lambda E: E.memset(self.eps_t[:], RMS_EPS), writes=["eps_t"])

    def hook(self, name):
        f = getattr(self, "hooks", {}).get(name)
        if f is not None:
            f()

    def bank(self):
        i = self.bank_rr
        self.bank_rr = (self.bank_rr + 1) % 8
        return i


def rmsnorm_T(cx, xT, xkey, gcol, hT, hkey, t0, t1, scr, tag):
    P = cx.P
    for b0 in range(t0, t1, 512):
        b1 = min(b0 + 512, t1)
        n = b1 - b0
        sq, acc, rstd = scr["sq"], scr["acc"], scr["rstd"]
        for s0 in range(0, n, 256):
            m = min(256, n - s0)
            P.add("act", lambda E, b0=b0, s0=s0, m=m: E.activation(out=sq[:, :, 0:m], in_=xT[:, :, b0 + s0:b0 + s0 + m], func=AF.Square),
                  reads=[(xkey, b0 // 512)], writes=["sq"])
            P.add("dve", lambda E, s0=s0, m=m: E.reduce_sum(out=acc[:, s0:s0 + m], in_=sq[:, :, 0:m].rearrange("p c t -> p t c"), axis=AX.X),
                  reads=["sq"], writes=["acc"])
        bi = cx.bank()
        bk = cx.banks[bi]
        P.add("pe", lambda E, n=n, bk=bk: E.matmul(bk[:, 0:n], lhsT=cx.ones_f[:], rhs=acc[:, 0:n], start=True, stop=True),
              reads=["acc", "ones_f"], writes=[("bank", bi)])
        P.add("act", lambda E, n=n, bk=bk: E.activation(out=rstd[:, 0:n], in_=bk[:, 0:n], func=AF.Sqrt, bias=cx.eps_t[:, 0:1], scale=1.0),
              reads=[("bank", bi), "eps_t"], writes=["rstd"])
        P.add("dve", lambda E, n=n: E.reciprocal(out=rstd[:, 0:n], in_=rstd[:, 0:n]),
              reads=["rstd"], writes=["rstd"])
        for c in range(8):
            eng = "dve"
            P.add(eng, lambda E, c=c, b0=b0, b1=b1, n=n: E.scalar_tensor_tensor(
                out=hT[:, c, b0 - t0:b1 - t0], in0=xT[:, c, b0:b1], scalar=gcol(c), in1=rstd[:, 0:n],
                op0=ALU.mult, op1=ALU.mult),
                reads=[(xkey, b0 // 512), "rstd", "gains"], writes=[(hkey, (b0 - t0) // 512)])


def mlp_T(cx, layer, xT, w1, w2, gcol, scr, w1k=(), w2k=()):
    P = cx.P
    NT = cx.NT
    W2 = scr["W2"]
    W1b = scr["W1b"]
    aT = scr["aT"]
    hT = scr["hT"]
    rl = scr["rl"]
    w2v = w2.rearrange("(f p) o -> p f o", p=128)
    w1v = w1.rearrange("(c p) f -> p c f", p=128)
    for q in range(8):
        P.dma("sp", W2[:, q * 4:(q + 1) * 4, :], w2v[:, q * 4:(q + 1) * 4, :], reads=list(w2k), writes=[("W2", q)], stream="w2")
    g_i = 0
    for tb in range(NT // 512):
        t0 = tb * 512
        rmsnorm_T(cx, xT, "xT", gcol, hT, "hT", t0, t0 + 512, scr, "mlp")
        for g in range(8):
            wb = W1b[g_i % 2]
            wk = ("W1b", g_i % 2)
            P.dma("sp", wb[:], w1v[:, :, g * 512:(g + 1) * 512], reads=list(w1k), writes=[wk], stream=f"w1_{g_i % 2}")
            g_i += 1
            for j in range(4):
                f = g * 4 + j
                bi = cx.bank()
                bk = cx.banks[bi]
                for c in range(8):
                    P.add("pe", lambda E, c=c, j=j, wb=wb, bk=bk: E.matmul(
                        bk[:, :], lhsT=wb[:, c, j * 128:(j + 1) * 128], rhs=hT[:, c, :], start=(c == 0), stop=(c == 7)),
                        reads=[wk, ("hT", 0)], writes=[("bank", bi)])
                r = rl[f % 2]
                rk = ("rl", f % 2)
                P.add("act", lambda E, r=r, bk=bk: E.activation(out=r[:], in_=bk[:, :], func=AF.Relu),
                      reads=[("bank", bi)], writes=[rk])
                eng = "pool" if f % 2 == 0 else "dve"
                P.add(eng, lambda E, r=r, f=f: E.tensor_tensor(out=aT[:, f, :], in0=r[:], in1=r[:], op=ALU.mult),
                      reads=[rk], writes=[("aT", f)])
        for o in range(8):
            bi = cx.bank()
            bk = cx.banks[bi]
            for f in range(32):
                P.add("pe", lambda E, f=f, o=o, bk=bk: E.matmul(
                    bk[:, :], lhsT=W2[:, f, o * 128:(o + 1) * 128], rhs=aT[:, f, :], start=(f == 0), stop=(f == 31)),
                    reads=[("W2", q_) for q_ in range(8)] + [("aT", f)], writes=[("bank", bi)])
            P.add("dve", lambda E, o=o, t0=t0, bk=bk: E.tensor_tensor(
                out=xT[:, o, t0:t0 + 512], in0=xT[:, o, t0:t0 + 512], in1=bk[:, :], op=ALU.add),
                reads=[("bank", bi), ("xT", tb)], writes=[("xT", tb)])


def rms_seg(cx, src, skey, s0, n, dst, d0, dkeys, gcol, scr):
    P = cx.P
    sq, acc, rstd = scr["sq"], scr["acc"], scr["rstd"]
    for a0 in range(0, n, 256):
        m = min(256, n - a0)
        P.add("act", lambda E, a0=a0, m=m: E.activation(out=sq[:, :, 0:m], in_=src[:, :, s0 + a0:s0 + a0 + m], func=AF.Square),
              reads=[skey], writes=["sq"])
        P.add("dve", lambda E, a0=a0, m=m: E.reduce_sum(out=acc[:, a0:a0 + m], in_=sq[:, :, 0:m].rearrange("p c t -> p t c"), axis=AX.X),
              reads=["sq"], writes=["acc"])
    bi = cx.bank()
    bk = cx.banks[bi]
    P.add("pe", lambda E: E.matmul(bk[:, 0:n], lhsT=cx.ones_f[:], rhs=acc[:, 0:n], start=True, stop=True),
          reads=["acc", "ones_f"], writes=[("bank", bi)])
    P.add("act", lambda E: E.activation(out=rstd[:, 0:n], in_=bk[:, 0:n], func=AF.Sqrt, bias=cx.eps_t[:, 0:1], scale=1.0),
          reads=[("bank", bi), "eps_t"], writes=["rstd"])
    P.add("dve", lambda E: E.reciprocal(out=rstd[:, 0:n], in_=rstd[:, 0:n]), reads=["rstd"], writes=["rstd"])
    for c in range(8):
        P.add("dve", lambda E, c=c: E.scalar_tensor_tensor(
            out=dst[:, c, d0:d0 + n], in0=src[:, c, s0:s0 + n], scalar=gcol(c), in1=rstd[:, 0:n],
            op0=ALU.mult, op1=ALU.mult), reads=[skey, "rstd", "gains"], writes=list(dkeys))


def qknorm(cx, bk, bi, n, out_ap, okeys, gain_ap, scr, par):
    P = cx.P
    sqb = scr["sqb"][par]
    rs = scr["rs"][par]
    P.add("act", lambda E: E.activation(out=sqb[:, 0:n], in_=bk[:, 0:n], func=AF.Square),
          reads=[("bank", bi)], writes=[("sqb", par)])
    b2 = cx.bank()
    bk2 = cx.banks[b2]
    P.add("pe", lambda E: E.matmul(bk2[:, 0:n], lhsT=cx.blockones[:], rhs=sqb[:, 0:n], start=True, stop=True),
          reads=[("sqb", par), "blockones"], writes=[("bank", b2)])
    P.add("act", lambda E: E.activation(out=rs[:, 0:n], in_=bk2[:, 0:n], func=AF.Sqrt, bias=cx.eps_t[:, 0:1], scale=1.0),
          reads=[("bank", b2), "eps_t"], writes=[("rs", par)])
    P.add("dve", lambda E: E.reciprocal(out=rs[:, 0:n], in_=rs[:, 0:n]), reads=[("rs", par)], writes=[("rs", par)])
    P.add("dve", lambda E: E.scalar_tensor_tensor(out=out_ap, in0=bk[:, 0:n], scalar=gain_ap, in1=rs[:, 0:n],
                                                  op0=ALU.mult, op1=ALU.mult),
          reads=[("bank", bi), ("rs", par), "gqk"], writes=list(okeys))


def hkeys(lo, hi):
    return [("hT", j) for j in range(lo // 256, (hi + 255) // 256)]


def proj_feat(cx, hT, t0, ntok, w, wkey, col0, consumer):
    P = cx.P
    for b0 in range(0, ntok, 512):
        n = min(512, ntok - b0)
        bi = cx.bank()
        bk = cx.banks[bi]
        for c in range(8):
            P.add("pe", lambda E, c=c, b0=b0, n=n, bk=bk: E.matmul(
                bk[:, 0:n], lhsT=w[:, c, col0:col0 + 128], rhs=hT[:, c, t0 + b0:t0 + b0 + n], start=(c == 0), stop=(c == 7)),
                reads=(wkey if isinstance(wkey, list) else [wkey]) + hkeys(t0 + b0, t0 + b0 + n), writes=[("bank", bi)])
        consumer(bk, bi, b0, n)


def proj_tok(cx, hT, ntile, w, wkey, col0, ncols, consumer):
    P = cx.P
    for j in range(ntile):
        bi = cx.bank()
        bk = cx.banks[bi]
        for c in range(8):
            P.add("pe", lambda E, c=c, j=j, bk=bk: E.matmul(
                bk[:, 0:ncols], lhsT=hT[:, c, j * 128:(j + 1) * 128], rhs=w[:, c, col0:col0 + ncols], start=(c == 0), stop=(c == 7)),
                reads=(wkey if isinstance(wkey, list) else [wkey]) + hkeys(j * 128, (j + 1) * 128), writes=[("bank", bi)])
        consumer(bk, bi, j)


def outproj_chunk(cx, xT, Obuf, okey, ocol0, ntile, Wrows, wkey, OTm, ident):
    P = cx.P
    for n0 in range(0, ntile, 4):
        bi = cx.bank()
        bkb = cx.banks[bi][:, :].bitcast(BF16)
        for n in range(n0, n0 + 4):
            P.add("pe", lambda E, n=n, n0=n0, bkb=bkb: E.transpose(
                out=bkb[:, (n - n0) * 128:(n - n0 + 1) * 128], in_=Obuf[:, n, ocol0:ocol0 + 128], identity=ident[:]),
                reads=[(okey, n), "ident"], writes=[("bank", bi)])
        eng = "act" if (n0 // 4) % 2 == 0 else "dve"
        if eng == "act":
            P.add("act", lambda E, n0=n0, bkb=bkb: E.copy(out=OTm[:, n0 * 128:(n0 + 4) * 128], in_=bkb[:, 0:512]),
                  reads=[("bank", bi)], writes=[("OTm", n0 // 4)])
        else:
            P.add("dve", lambda E, n0=n0, bkb=bkb: E.tensor_copy(out=OTm[:, n0 * 128:(n0 + 4) * 128], in_=bkb[:, 0:512]),
                  reads=[("bank", bi)], writes=[("OTm", n0 // 4)])
    for o in range(8):
        for tb in range(ntile // 4):
            bi = cx.bank()
            bk = cx.banks[bi]
            P.add("pe", lambda E, o=o, tb=tb, bk=bk: E.matmul(bk[:, :], lhsT=Wrows(o), rhs=OTm[:, tb * 512:(tb + 1) * 512],
                                                              start=True, stop=True),
                  reads=[wkey, ("OTm", tb)], writes=[("bank", bi)])
            P.add("dve", lambda E, o=o, tb=tb, bk=bk: E.tensor_tensor(
                out=xT[:, o, tb * 512:(tb + 1) * 512], in0=xT[:, o, tb * 512:(tb + 1) * 512], in1=bk[:, :], op=ALU.add),
                reads=[("bank", bi), ("xT", tb)], writes=[("xT", tb)])


def attn_in_proj(cx, hT, w, wkey, qcols, kcol, vcol, nh, QT, KT, V, kval, gq, gk, scr, tag):
    P = cx.P
    par = [0]

    def nxt():
        par[0] ^= 1
        return par[0]

    for gi, (col0, dst) in enumerate(qcols):
        proj_feat(cx, hT, 256, 2048, w, wkey, col0,
                  lambda bk, bi, b0, n, dst=dst, gi=gi: qknorm(cx, bk, bi, n, dst(b0, n), [(tag + "QT", gi, b0 // 512)], gq, scr, nxt()))
    proj_feat(cx, hT, 0, 2560, w, wkey, kcol,
              lambda bk, bi, b0, n: qknorm(cx, bk, bi, n, KT[:, b0:b0 + n], [(tag + "KT", b0 // 512)], gk, scr, nxt()))
    Vv = V[:, :, :].rearrange("p j (h e) -> p j h e", e=65)
    for h in range(nh):
        P.add("dve", lambda E, h=h: E.tensor_copy(out=Vv[:, :, h, 64], in_=kval[:, :]), reads=["kval"], writes=[(tag + "Vones", h)])

    def vcons(bk, bi, j):
        P.add("act", lambda E: E.activation(out=Vv[:, j, :, 0:64], in_=bk[:, 0:nh * 64].rearrange("p (h d) -> p h d", d=64),
                                            func=AF.Copy, scale=kval[:, j:j + 1]),
              reads=[("bank", bi), "kval"], writes=[(tag + "V", j)])
    proj_tok(cx, hT, 20, w, wkey, vcol, nh * 64, vcons)


def even_mixer(cx, xT, D):
    P = cx.P
    nc = P.nc
    with P.scope():
        hT = P.sbuf([128, 8, 2560], BF16, "hT0")
        Wout = P.sbuf([128, 8, 1024], BF16, "Wout0")
        Mtab = P.sbuf([128, 3, 8, 128], BF16, "Mtab")
        gqk = P.sbuf([128, 4], F32, "gqk")
        esink = P.sbuf([128, 8], F32, "esink")
        kval = P.sbuf([128, 20], F32, "kval")
        gat = P.sbuf([128, 8], F32, "gat")
        OTm = P.sbuf([128, 2048], BF16, "OTm")
        scr = dict(sqb=[P.sbuf([128, 512], F32, f"sqb{i}") for i in range(2)],
                   rs=[P.sbuf([128, 512], F32, f"rs{i}") for i in range(2)])
        wA_ = P.sbuf([128, 8, 768], BF16, "wA")
        wkeyA = ("wAall",)
        P.dma("pool", wA_[:, :, :], D["win"].rearrange("(c p) f -> p c f", p=128)[:, :, 0:768], writes=[wkeyA], stream="wA")
        wov32 = D["wout"].rearrange("(m p) o -> p m o", p=128)
        for q in range(2):
            P.dma("pool", Wout[:, q * 4:(q + 1) * 4, :], wov32[:, q * 4:(q + 1) * 4, :], writes=[("Wout", q)], stream=f"wout{q}")
        P.dma("sp", gqk[:], D["gqk"], writes=["gqk"], stream="c_gqk")
        P.dma("sp", esink[:], D["sink"], writes=["esink"], stream="c_sink")
        P.dma("sp", kval[:], D["kval"], writes=["kval"], stream="c_kval")
        P.dma("sp", gat[:], D["g_attn"].rearrange("(c p) -> p c", p=128), writes=["gains"], stream="c_gat",
              allow_slow_non_contiguous=True)
        P.add("act", lambda E: E.activation(out=esink[:], in_=esink[:], func=AF.Exp), reads=["esink"], writes=["esink"])
        with P.scope():
            xh = P.sbuf([128, 8, 512], F32, "xh")
            rscr = dict(sq=P.sbuf([128, 8, 256], F32, "sq"), acc=P.sbuf([128, 512], F32, "acc"), rstd=P.sbuf([128, 512], F32, "rstd"))
            mst = P.sbuf([128, 3 * 8 * 128], F32, "mst")
            P.dma("sp", xh[:], D["xh"].rearrange("(c p) t -> p c t", p=128), writes=["xh"], stream="c_xh")
            P.dma("sp", mst[:], D["gqa_tab"], writes=["mst"], stream="c_mst")
            cx.hook("h0")
            P.add("act", lambda E: E.activation(out=Mtab[:, :, :, :].rearrange("p a h q -> p (a h q)"), in_=mst[:], func=AF.Exp),
                  reads=["mst"], writes=["Mtab"])
            gcol = lambda c: gat[:, c:c + 1]
            rms_seg(cx, xh, "xh", 0, 256, hT, 0, hkeys(0, 256), gcol, rscr)
            rms_seg(cx, xh, "xh", 256, 256, hT, 2304, hkeys(2304, 2560), gcol, rscr)
            for tb in range(4):
                rms_seg(cx, xT, ("xT", tb), tb * 512, 512, hT, 256 + tb * 512, hkeys(256 + tb * 512, 768 + tb * 512), gcol, rscr)
        winv = D["win_bf"].rearrange("(c p) f -> p c f", p=128)
        with P.scope():
            w = wA_
            QT = P.sbuf([128, 4, 2048], BF16, "QTa")
            KT = P.sbuf([128, 2560], BF16, "KTa")
            V = P.sbuf([128, 20, 130], BF16, "Va")
            Ob = P.sbuf([128, 16, 512], BF16, "ObA")
            Es = [P.sbuf([128, 512], F32, f"EsA{i}") for i in range(3)]
            PT = [P.sbuf([128, 512], BF16, f"PTA{i}") for i in range(3)]
            den = P.sbuf([128, 4], F32, "denA")
            attn_in_proj(cx, hT, w, wkeyA,
                         [(g * 128, (lambda b0, n, g=g: QT[:, g, b0:b0 + n])) for g in range(4)],
                         512, 640, 2, QT, KT, V, kval, gqk[:, 0:1], gqk[:, 1:2], scr, "A")
            cx.hook("h1")
            for n in range(16):
                for kv in range(2):
                    lo, hi = kv * 64, (kv + 1) * 64
                    for rp in range(3):
                        j = n + 1 + rp
                        bi = cx.bank()
                        bk = cx.banks[bi]
                        P.add("pe", lambda E, j=j, bk=bk, lo=lo, hi=hi, n=n: E.matmul(
                            bk[:, :], lhsT=KT[lo:hi, j * 128:(j + 1) * 128], rhs=QT[lo:hi, :, n * 128:(n + 1) * 128],
                            start=True, stop=True),
                            reads=[("AKT", j // 4)] + [("AQT", g, n // 4) for g in range(4)], writes=[("bank", bi)])
                        P.add("act", lambda E, rp=rp, bk=bk: E.activation(out=Es[rp][:], in_=bk[:, :], func=AF.Exp, scale=0.125),
                              reads=[("bank", bi)], writes=[("EsA", rp)])
                        eng = "pool" if rp != 1 else "dve"
                        P.add(eng, lambda E, rp=rp, kv=kv: E.tensor_tensor(
                            out=PT[rp][:], in0=Es[rp][:], in1=Mtab[:, rp, kv * 4:(kv + 1) * 4, :].rearrange("p h q -> p (h q)"),
                            op=ALU.mult), reads=[("EsA", rp), "Mtab"], writes=[("PTA", rp)])
                    bo = cx.bank()
                    bko = cx.banks[bo]
                    for g in range(4):
                        for rp in range(3):
                            j = n + 1 + rp
                            P.add("pe", lambda E, g=g, rp=rp, j=j, kv=kv, bko=bko: E.matmul(
                                bko[:, g * 65:(g + 1) * 65], lhsT=PT[rp][:, g * 128:(g + 1) * 128], rhs=V[:, j, kv * 65:(kv + 1) * 65],
                                start=(rp == 0), stop=(rp == 2)),
                                reads=[("PTA", rp), ("AV", j)] + [("AVones", h) for h in range(2)], writes=[("bank", bo)])
                    bov = bko[:, 0:260].rearrange("p (g e) -> p g e", e=65)
                    P.add("dve", lambda E, kv=kv, bov=bov: E.tensor_tensor(out=den[:, :], in0=bov[:, :, 64], in1=esink[:, kv * 4:(kv + 1) * 4],
                                                                           op=ALU.add),
                          reads=[("bank", bo), "esink"], writes=["denA"])
                    P.add("dve", lambda E: E.reciprocal(out=den[:, :], in_=den[:, :]), reads=["denA"], writes=["denA"])
                    for g in range(4):
                        P.add("act", lambda E, g=g, kv=kv, n=n, bko=bko: E.activation(
                            out=Ob[:, n, (kv * 4 + g) * 64:(kv * 4 + g + 1) * 64], in_=bko[:, g * 65:g * 65 + 64], func=AF.Copy,
                            scale=den[:, g:g + 1]), reads=[("bank", bo), "denA"], writes=[("ObA", n)])
            cx.hook("h2")
            for m in range(4):
                outproj_chunk(cx, xT, Ob, "ObA", m * 128, 16, (lambda o, m=m: Wout[:, m, o * 128:(o + 1) * 128]), ("Wout", 0), OTm, cx.ident)
        with P.scope():
            ws = [P.sbuf([128, 8, 384], BF16, f"wB{i}") for i in range(2)]
            QTn = P.sbuf([128, 2048], BF16, "QTb")
            KTn = P.sbuf([128, 2560], BF16, "KTb")
            Vn = P.sbuf([128, 20, 130], BF16, "Vb")
            Obn = P.sbuf([128, 16, 128], BF16, "ObB")
            tab = P.sbuf([128, 5, 2, 14, 64], BF16, "tabB")
            Esn = [P.sbuf([128, 512], F32, f"EsB{i}") for i in range(2)]
            PTn = [P.sbuf([128, 512], BF16, f"PTB{i}") for i in range(2)]
            denn = P.sbuf([128, 2], F32, "denB")
            for hp in range(4):
                w = ws[hp % 2]
                wk = [("wB", hp % 2, qi) for qi in range(3)]
                for qi in range(3):
                    c0 = 768 + qi * 512 + hp * 128
                    P.dma("sp", w[:, :, qi * 128:(qi + 1) * 128], winv[:, :, c0:c0 + 128], reads=D["win_k"], writes=[wk[qi]], stream=f"wB{hp % 2}")
                for v in range(5):
                    P.dma("pool", tab[:, v, :, :, :], D["na_tab"][v, :, 2 * hp:2 * hp + 2, :, :], writes=[("tabB", v), ("tabBx", v)], stream="tabB")
                P.add("act", lambda E: E.activation(out=tab[:, :, :, :, :].rearrange("p v h s q -> p (v h s q)"),
                                                    in_=tab[:, :, :, :, :].rearrange("p v h s q -> p (v h s q)"), func=AF.Exp),
                      reads=[("tabB", v) for v in range(5)], writes=[("tabBx", v) for v in range(5)])
                attn_in_proj(cx, hT, w, wk, [(0, (lambda b0, n: QTn[:, b0:b0 + n]))], 128, 256, 2, QTn, KTn, Vn, kval,
                             gqk[:, 2:3], gqk[:, 3:4], scr, "B")
                for t in range(16):
                    u0, u1 = (0, 5) if t == 0 else ((-1, 4) if t == 15 else (0, 4))
                    var = {0: 1, 1: 2, 14: 3, 15: 4}.get(t, 0)
                    nU = u1 - u0 + 1
                    bo = cx.bank()
                    bko = cx.banks[bo]
                    for hh in range(2):
                        lo, hi = hh * 64, (hh + 1) * 64
                        sb = [cx.bank(), cx.bank()]
                        for ui in range(nU):
                            j = t + u0 + ui
                            bi = sb[ui // 4]
                            bk = cx.banks[bi]
                            P.add("pe", lambda E, ui=ui, j=j, bk=bk, lo=lo, hi=hi, t=t: E.matmul(
                                bk[:, (ui % 4) * 128:(ui % 4 + 1) * 128], lhsT=KTn[lo:hi, j * 128:(j + 1) * 128],
                                rhs=QTn[lo:hi, t * 128:(t + 1) * 128], start=True, stop=True),
                                reads=[("BKT", j // 4), ("BQT", 0, t // 4)], writes=[("bank", bi)])
                        for part in range(2):
                            cnt = min(4, nU - part * 4)
                            if cnt <= 0:
                                continue
                            bi = sb[part]
                            bk = cx.banks[bi]
                            s0 = 2 * (u0 + part * 4 + 1)
                            P.add("act", lambda E, part=part, cnt=cnt, bk=bk: E.activation(
                                out=Esn[part][:, 0:cnt * 128], in_=bk[:, 0:cnt * 128], func=AF.Exp, scale=0.125),
                                reads=[("bank", bi)], writes=[("EsB", part)])
                            eng = "pool" if part == 0 else "dve"
                            P.add(eng, lambda E, part=part, cnt=cnt, s0=s0, hh=hh, var=var: E.tensor_tensor(
                                out=PTn[part][:, 0:cnt * 128], in0=Esn[part][:, 0:cnt * 128],
                                in1=tab[:, var, hh, s0:s0 + 2 * cnt, :].rearrange("p s q -> p (s q)"), op=ALU.mult),
                                reads=[("EsB", part), ("tabBx", var)], writes=[("PTB", part)])
                        for ui in range(nU):
                            j = t + u0 + ui
                            P.add("pe", lambda E, ui=ui, j=j, hh=hh, bko=bko, nU=nU: E.matmul(
                                bko[:, hh * 65:(hh + 1) * 65], lhsT=PTn[ui // 4][:, (ui % 4) * 128:(ui % 4 + 1) * 128],
                                rhs=Vn[:, j, hh * 65:(hh + 1) * 65], start=(ui == 0), stop=(ui == nU - 1)),
                                reads=[("PTB", ui // 4), ("BV", j)] + [("BVones", h) for h in range(2)], writes=[("bank", bo)])
                    bov = bko[:, 0:130].rearrange("p (g e) -> p g e", e=65)
                    P.add("dve", lambda E, bov=bov: E.reciprocal(out=denn[:, :], in_=bov[:, :, 64]), reads=[("bank", bo)], writes=["denB"])
                    for hh in range(2):
                        P.add("act", lambda E, hh=hh, t=t, bko=bko: E.activation(
                            out=Obn[:, t, hh * 64:(hh + 1) * 64], in_=bko[:, hh * 65:hh * 65 + 64], func=AF.Copy, scale=denn[:, hh:hh + 1]),
                            reads=[("bank", bo), "denB"], writes=[("ObB", t)])
                cx.hook("h%d" % (3 + hp))
                m = 4 + hp
                outproj_chunk(cx, xT, Obn, "ObB", 0, 16, (lambda o, m=m: Wout[:, m, o * 128:(o + 1) * 128]), ("Wout", 1), OTm, cx.ident)


GN_EPS = 1e-5
import os
STG = int(os.environ.get('RM_STAGE', '9'))
KSCALE = 128.0 ** -0.5


def ret_tables(cx, D, T):
    P = cx.P
    lg, u, tt = T["lg"], T["u"], T["tt"]
    P.dma("sp", lg[:], D["dec"], writes=["lg"], stream="c_lg")
    P.dma("sp", T["cols"][:], D["rcols"], writes=["rcols"], stream="c_rcols")
    P.dma("sp", T["Pf"][:], D["Pf"], writes=["Pf"], stream="c_Pf")
    P.dma("sp", T["Pb"][:], D["Pb"], writes=["Pb"], stream="c_Pb")
    P.add("act", lambda E: E.activation(out=u[:], in_=lg[:], func=AF.Exp, scale=-1.0), reads=["lg"], writes=["u"])
    coef = [(-0.2, 0.25), (-1.0, 1.0 / 3.0), (-1.0, 0.5), (-1.0, 1.0)]
    P.add("dve", lambda E: E.tensor_scalar(out=tt[:], in0=u[:], scalar1=-0.2, scalar2=0.25, op0=ALU.mult, op1=ALU.add),
          reads=["u"], writes=["tt"])
    for (a, b) in coef[1:]:
        P.add("dve", lambda E: E.tensor_tensor(out=tt[:], in0=tt[:], in1=u[:], op=ALU.mult), reads=["tt", "u"], writes=["tt"])
        P.add("dve", lambda E, a=a, b=b: E.tensor_scalar(out=tt[:], in0=tt[:], scalar1=a, scalar2=b, op0=ALU.mult, op1=ALU.add),
              reads=["tt"], writes=["tt"])
    P.add("dve", lambda E: E.tensor_tensor(out=tt[:], in0=tt[:], in1=u[:], op=ALU.mult), reads=["tt", "u"], writes=["tt"])
    P.add("dve", lambda E: E.tensor_scalar(out=lg[:], in0=tt[:], scalar1=-1.0, scalar2=None, op0=ALU.mult), reads=["tt"], writes=["lg"])
    for name, ci, half in [("CF", 0, 0), ("CB", 1, 1), ("ZF", 2, 0), ("ZB", 3, 1)]:
        t = T[name]
        P.add("dve", lambda E, t=t, ci=ci, half=half: E.tensor_scalar(out=t[:], in0=lg[:, half * 8:(half + 1) * 8],
                                                                      scalar1=T["cols"][:, ci:ci + 1], scalar2=None, op0=ALU.mult),
              reads=["lg", "rcols"], writes=[name])
        P.add("act", lambda E, t=t: E.activation(out=t[:], in_=t[:], func=AF.Exp), reads=[name], writes=[name])
    P.add("act", lambda E: E.activation(out=T["cd"][:], in_=lg[:], func=AF.Exp, scale=128.0), reads=["lg"], writes=["cd"])
    if "DmT" in T:
        for h in range(8):
            P.add("dve", lambda E, h=h: E.tensor_scalar(out=T["dtmp"][:], in0=T["Pf"][:], scalar1=lg[:, h:h + 1], scalar2=None, op0=ALU.mult),
                  reads=["lg", "Pf"], writes=["dtmp"])
            P.add("dve", lambda E, h=h: E.scalar_tensor_tensor(out=T["dtmp"][:], in0=T["Pb"][:], scalar=lg[:, 8 + h:9 + h], in1=T["dtmp"][:],
                                                               op0=ALU.mult, op1=ALU.add), reads=["lg", "Pb", "dtmp"], writes=["dtmp"])
            P.add("act", lambda E, h=h: E.activation(out=T["DmT"][:, h, :], in_=T["dtmp"][:], func=AF.Exp), reads=["dtmp"], writes=[("DmT", h)])


def ret_alloc_tables(P, full):
    T = dict(lg=P.sbuf([128, 16], F32, "lg"),
             CF=P.sbuf([128, 8], F32, "CF"), CB=P.sbuf([128, 8], F32, "CB"), ZF=P.sbuf([128, 8], F32, "ZF"),
             ZB=P.sbuf([128, 8], F32, "ZB"), cd=P.sbuf([128, 16], F32, "cd"))
    T["DmT"] = P.sbuf([128, 8, 128], BF16, "DmT")
    T["CFr"] = P.sbuf([128, 8, 128], BF16, "CFr")
    T["CBr"] = P.sbuf([128, 8, 128], BF16, "CBr")
    T["ZFn"] = P.sbuf([128, 8, 16], F32, "ZFn")
    T["ZBn"] = P.sbuf([128, 8, 16], F32, "ZBn")
    return T


def ret_alloc_tmp(P, T):
    T.update(dict(u=P.sbuf([128, 16], F32, "u"), tt=P.sbuf([128, 16], F32, "tt"),
                  cols=P.sbuf([128, 4], F32, "rcols"), Pf=P.sbuf([128, 128], F32, "Pf"), Pb=P.sbuf([128, 128], F32, "Pb"),
                  dtmp=P.sbuf([128, 128], F32, "dtmp"), irow=P.sbuf([128, 256], F32, "irow"), e1=P.sbuf([128, 32], F32, "e1"),
                  CFr32=P.sbuf([128, 8, 128], F32, "CFr32"), CBr32=P.sbuf([128, 8, 128], F32, "CBr32")))


def ret_kv(cx, hT, w, wkeys, kcol, vcol, KT, V, Kzf, Kzb, T, h):
    P = cx.P
    flip = [0]

    def kcons(bk, bi, b0, n):
        P.add("act", lambda E: E.activation(out=KT[:, b0:b0 + n], in_=bk[:, 0:n], func=AF.Copy, scale=KSCALE),
              reads=[("bank", bi)], writes=[("KT", b0 // 512)])
    proj_feat(cx, hT, 0, 2048, w, wkeys, kcol, kcons)

    def vcons(bk, bi, j):
        flip[0] ^= 1
        if flip[0]:
            P.add("act", lambda E: E.copy(out=V[:, j, :], in_=bk[:, 0:256]), reads=[("bank", bi)], writes=[("V", j)])
        else:
            P.add("dve", lambda E: E.tensor_copy(out=V[:, j, :], in_=bk[:, 0:256]), reads=[("bank", bi)], writes=[("V", j)])
    proj_tok(cx, hT, 16, w, wkeys, vcol, 256, vcons)
    for n0 in range(0, 16, 4):
        bi = cx.bank()
        bkb = cx.banks[bi][:, :].bitcast(BF16)
        for n in range(n0, n0 + 4):
            P.add("pe", lambda E, n=n, n0=n0, bkb=bkb: E.transpose(out=bkb[:, (n - n0) * 128:(n - n0 + 1) * 128],
                                                                   in_=KT[:, n * 128:(n + 1) * 128], identity=cx.ident[:]),
                  reads=[("KT", n // 4), "ident"], writes=[("bank", bi)])
        P.add("act", lambda E, n0=n0, bkb=bkb: E.activation(out=Kzf[:, n0:n0 + 4, :].rearrange("p n d -> p (n d)"), in_=bkb[:, 0:512],
                                                            func=AF.Copy, scale=T["ZF"][:, h:h + 1]),
              reads=[("bank", bi), "ZF"], writes=[("Kzf", n0 // 4)])
        P.add("dve", lambda E, n0=n0, bkb=bkb: E.tensor_scalar(out=Kzb[:, n0:n0 + 4, :].rearrange("p n d -> p (n d)"), in0=bkb[:, 0:512],
                                                               scalar1=T["ZB"][:, h:h + 1], scalar2=None, op0=ALU.mult),
              reads=[("bank", bi), "ZB", ("Kzf", n0 // 4)], writes=[("Kzb", n0 // 4)])


def ret_state_step(cx, Kz, kzkey, V, n, R, rkey, cdcol):
    P = cx.P
    bi = cx.bank()
    bk = cx.banks[bi]
    P.add("pe", lambda E: E.matmul(bk[:, 0:256], lhsT=Kz[:, n, :], rhs=V[:, n, :], start=True, stop=True),
          reads=[(kzkey, n // 4), ("V", n)], writes=[("bank", bi)])
    P.add("dve", lambda E: E.scalar_tensor_tensor(out=R[:], in0=R[:], scalar=cdcol, in1=bk[:, 0:256], op0=ALU.mult, op1=ALU.add),
          reads=[("bank", bi), rkey, "cd"], writes=[rkey])


def ret_prologue(cx, xT, D, hT, T):
    P = cx.P
    with P.scope():
        gr = P.sbuf([128, 8], F32, "gr")
        P.dma("sp", gr[:], D["g_ret"].rearrange("(c p) -> p c", p=128), writes=["gains"], stream="c_gr", allow_slow_non_contiguous=True)
        rscr = dict(sq=P.sbuf([128, 8, 256], F32, "sq"), acc=P.sbuf([128, 512], F32, "acc"), rstd=P.sbuf([128, 512], F32, "rstd"))
        for tb in range(4):
            rms_seg(cx, xT, ("xT", tb), tb * 512, 512, hT, tb * 512, hkeys(tb * 512, tb * 512 + 512), lambda c: gr[:, c:c + 1], rscr)
    with P.scope():
        ret_alloc_tmp(P, T)
        ret_tables(cx, D, T)
        ret_tables2(cx, D, T)


def ret_tables2(cx, D, T2):
    P = cx.P
    lg = T2["lg"]
    P.dma("sp", T2["irow"][:], D["irow"], writes=["irow"], stream="c_irow")
    P.dma("sp", T2["e1"][:], D["e1"], writes=["e1"], stream="c_e1")
    for h in range(8):
        P.add("dve", lambda E, h=h: E.tensor_scalar(out=T2["CFr32"][:, h, :], in0=T2["irow"][:, 0:128], scalar1=lg[:, h:h + 1], scalar2=None, op0=ALU.mult),
              reads=["lg", "irow"], writes=["CFr"])
        P.add("dve", lambda E, h=h: E.tensor_scalar(out=T2["CBr32"][:, h, :], in0=T2["irow"][:, 128:256], scalar1=lg[:, 8 + h:9 + h], scalar2=None, op0=ALU.mult),
              reads=["lg", "irow"], writes=["CBr"])
        P.add("dve", lambda E, h=h: E.tensor_scalar(out=T2["ZFn"][:, h, :], in0=T2["e1"][:, 0:16], scalar1=lg[:, h:h + 1], scalar2=None, op0=ALU.mult),
              reads=["lg", "e1"], writes=["ZFn"])
        P.add("dve", lambda E, h=h: E.tensor_scalar(out=T2["ZBn"][:, h, :], in0=T2["e1"][:, 16:32], scalar1=lg[:, 8 + h:9 + h], scalar2=None, op0=ALU.mult),
              reads=["lg", "e1"], writes=["ZBn"])
    for nm, src in [("CFr", "CFr32"), ("CBr", "CBr32"), ("ZFn", "ZFn"), ("ZBn", "ZBn")]:
        t = T2[nm]
        t0 = T2[src]
        P.add("act", lambda E, t=t, t0=t0: E.activation(out=t[:, :, :].rearrange("p h i -> p (h i)"), in_=t0[:, :, :].rearrange("p h i -> p (h i)"), func=AF.Exp),
              reads=[nm], writes=[nm])


def ret_kT_transposed(cx, KT, n0, dst, dkey, scale_fn, eng):
    P = cx.P
    bi = cx.bank()
    bkb = cx.banks[bi][:, :].bitcast(BF16)
    for n in range(n0, n0 + 4):
        P.add("pe", lambda E, n=n: E.transpose(out=bkb[:, (n - n0) * 128:(n - n0 + 1) * 128], in_=KT[:, n * 128:(n + 1) * 128], identity=cx.ident[:]),
              reads=[("KT", n // 4), "ident"], writes=[("bank", bi)])
    for n in range(n0, n0 + 4):
        if eng == "act":
            P.add("act", lambda E, n=n: E.activation(out=dst[:, n, :], in_=bkb[:, (n - n0) * 128:(n - n0 + 1) * 128], func=AF.Copy, scale=scale_fn(n)),
                  reads=[("bank", bi), "ZFn", "ZBn", "ZF", "ZB"], writes=[(dkey, n // 4)])
        else:
            P.add("dve", lambda E, n=n: E.tensor_scalar(out=dst[:, n, :], in0=bkb[:, (n - n0) * 128:(n - n0 + 1) * 128], scalar1=scale_fn(n), scalar2=None, op0=ALU.mult),
                  reads=[("bank", bi), "ZFn", "ZBn", "ZF", "ZB"], writes=[(dkey, n // 4)])


def ret_pass1(cx, hT, T, D, Tout):
    P = cx.P
    with P.scope():
        ws = [P.sbuf([128, 8, 384], BF16, f"w1p{i}") for i in range(2)]
        KT = P.sbuf([128, 2048], BF16, "KT1")
        V = P.sbuf([128, 16, 256], BF16, "V1")
        Kwf = P.sbuf([128, 16, 128], BF16, "Kwf")
        Kwb = P.sbuf([128, 16, 128], BF16, "Kwb")
        Ts = [P.sbuf([128, 256], F32, f"Ts{i}") for i in range(2)]
        winv = D["wino_bf"].rearrange("(c p) f -> p c f", p=128)
        for h in range(8):
            w = ws[h % 2]
            wk = [("w1p", h % 2, 0), ("w1p", h % 2, 1)]
            P.dma("sp", w[:, :, 0:128], winv[:, :, 1024 + h * 128:1024 + (h + 1) * 128], reads=D["wino_k"], writes=[wk[0]], stream=f"w1p{h % 2}")
            P.dma("sp", w[:, :, 128:384], winv[:, :, 2048 + h * 256:2048 + (h + 1) * 256], reads=D["wino_k"], writes=[wk[1]], stream=f"w1p{h % 2}")

            def kcons(bk, bi, b0, n):
                P.add("act", lambda E: E.activation(out=KT[:, b0:b0 + n], in_=bk[:, 0:n], func=AF.Copy, scale=KSCALE),
                      reads=[("bank", bi)], writes=[("KT", b0 // 512)])
            proj_feat(cx, hT, 0, 2048, w, wk, 0, kcons)
            flip = [0]

            def vcons(bk, bi, j):
                flip[0] ^= 1
                if flip[0]:
                    P.add("act", lambda E: E.copy(out=V[:, j, :], in_=bk[:, 0:256]), reads=[("bank", bi)], writes=[("V", j)])
                else:
                    P.add("dve", lambda E: E.tensor_copy(out=V[:, j, :], in_=bk[:, 0:256]), reads=[("bank", bi)], writes=[("V", j)])
            proj_tok(cx, hT, 16, w, wk, 128, 256, vcons)
            for n0 in range(0, 16, 4):
                ret_kT_transposed(cx, KT, n0, Kwf, "Kwf", (lambda n, h=h: T["ZFn"][:, h, n:n + 1]), "act")
                ret_kT_transposed(cx, KT, n0, Kwb, "Kwb", (lambda n, h=h: T["ZBn"][:, h, n:n + 1]), "dve")
            for di, (Kw, kk) in enumerate([(Kwf, "Kwf"), (Kwb, "Kwb")]):
                bi = cx.bank()
                bk = cx.banks[bi]
                for n in range(16):
                    P.add("pe", lambda E, n=n, Kw=Kw, bk=bk: E.matmul(bk[:, 0:256], lhsT=Kw[:, n, :], rhs=V[:, n, :], start=(n == 0), stop=(n == 15)),
                          reads=[(kk, n // 4), ("V", n)], writes=[("bank", bi)])
                Tsb = Ts[di]
                P.add("act", lambda E, Tsb=Tsb, bk=bk: E.copy(out=Tsb[:], in_=bk[:, 0:256]), reads=[("bank", bi)], writes=[("Ts", di)])
                P.dma("sp", Tout[:, di, h, :], Tsb[:], reads=[("Ts", di)], writes=[("ToutD", di, h)], stream=f"tout{di}")


def ret_main(cx, xT, hT, T, D, Tall):
    P = cx.P
    with P.scope():
        lg = T["lg"]
        rcw = P.sbuf([128, 32], F32, "rcw")
        wf = P.sbuf([128, 8, 8], F32, "wf")
        wb = P.sbuf([128, 8, 8], F32, "wb")
        P.dma("sp", rcw[:], D["rcw"], writes=["rcw"], stream="c_rcw")
        for (wt, nm, half, eo, mo) in [(wf, "wf", 0, 0, 8), (wb, "wb", 1, 16, 24)]:
            for cp in range(8):
                P.add("dve", lambda E, wt=wt, cp=cp, half=half, eo=eo: E.tensor_scalar(
                    out=wt[:, cp, :], in0=lg[:, half * 8:(half + 1) * 8], scalar1=rcw[:, eo + cp:eo + cp + 1], scalar2=None, op0=ALU.mult),
                    reads=["lg", "rcw"], writes=[nm])
            P.add("act", lambda E, wt=wt: E.activation(out=wt[:, :, :].rearrange("p c h -> p (c h)"), in_=wt[:, :, :].rearrange("p c h -> p (c h)"),
                                                       func=AF.Exp), reads=[nm], writes=[nm])
            for cp in range(8):
                P.add("dve", lambda E, wt=wt, cp=cp, mo=mo: E.tensor_scalar(
                    out=wt[:, cp, :], in0=wt[:, cp, :], scalar1=rcw[:, mo + cp:mo + cp + 1], scalar2=None, op0=ALU.mult),
                    reads=[nm, "rcw"], writes=[nm])
        w = P.sbuf([128, 8, 768], BF16, "w1m")
        QT = P.sbuf([128, 2048], BF16, "QT1")
        KT = P.sbuf([128, 2048], BF16, "KT1")
        Qf = P.sbuf([128, 2048], BF16, "Qf1")
        Qb = P.sbuf([128, 2048], BF16, "Qb1")
        V = P.sbuf([128, 16, 256], BF16, "V1")
        G = P.sbuf([128, 16, 256], BF16, "G1")
        Kzf = P.sbuf([128, 16, 128], BF16, "Kzf")
        Kzb = P.sbuf([128, 16, 128], BF16, "Kzb")
        Rbs = P.sbuf([128, 16, 256], BF16, "Rbs")
        Rfs = P.sbuf([128, 16, 256], BF16, "Rfs")
        ATs = P.sbuf([128, 16, 128], BF16, "ATs")
        Z = P.sbuf([128, 16, 256], BF16, "Z1")
        Tal = P.sbuf([128, 4, 256], F32, "Tal")
        Rf = P.sbuf([128, 256], F32, "Rf")
        Rb = P.sbuf([128, 256], F32, "Rb")
        yq = P.sbuf([128, 4, 256], F32, "yq")
        sqq = P.sbuf([128, 4, 256], BF16, "sqq")
        st = P.sbuf([128, 8, 4], F32, "gnst")
        GN = P.sbuf([128, 256], F32, "GNh")
        Wo = P.sbuf([128, 2, 1024], BF16, "Wo1")
        OTm = P.sbuf([128, 2048], BF16, "OTm1")
        epsg = P.sbuf([128, 1], F32, "epsg")
        P.add("pool", lambda E: E.memset(epsg[:], GN_EPS), writes=["epsg"])
        winv = D["wino_bf"].rearrange("(c p) f -> p c f", p=128)
        wov = D["wouto_bf"].rearrange("(m p) o -> p m o", p=128)
        QT3 = QT[:, :].rearrange("p (n i) -> p n i", i=128)
        for h in range(8):
            wk = [("w1m", i) for i in range(4)]
            for i, (c0, c1, d0) in enumerate([(h * 128, (h + 1) * 128, 0), (1024 + h * 128, 1024 + (h + 1) * 128, 128),
                                               (2048 + h * 256, 2048 + (h + 1) * 256, 256), (4096 + h * 256, 4096 + (h + 1) * 256, 512)]):
                P.dma("sp", w[:, :, d0:d0 + (c1 - c0)], winv[:, :, c0:c1], reads=D["wino_k"], writes=[wk[i]], stream="w1m")
            P.dma("sp", Wo[:, :, :], wov[:, 2 * h:2 * h + 2, :], reads=D["wouto_k"], writes=["Wo1"], stream="wo1")
            P.dma("sp", GN[:], D["gn_rep"][:, h * 256:(h + 1) * 256], writes=["GNh"], stream="gn")
            for (R, rk, wt, nm, di) in [(Rf, "Rf", wf, "wf", 0), (Rb, "Rb", wb, "wb", 1)]:
                for c0 in (0, 4):
                    P.dma("sp", Tal[:, :, :], Tall[c0:c0 + 4, :, di, h, :].rearrange("c p v -> p c v"), reads=["TallD"], writes=["Tal"], stream="tal")
                    for cq in range(4):
                        cp = c0 + cq
                        if cp == 0:
                            P.add("dve", lambda E, R=R, wt=wt, h=h: E.tensor_scalar(out=R[:], in0=Tal[:, 0, :], scalar1=wt[:, 0, h:h + 1], scalar2=None, op0=ALU.mult),
                                  reads=["Tal", nm], writes=[rk])
                        else:
                            P.add("dve", lambda E, R=R, wt=wt, h=h, cp=cp, cq=cq: E.scalar_tensor_tensor(
                                out=R[:], in0=Tal[:, cq, :], scalar=wt[:, cp, h:h + 1], in1=R[:], op0=ALU.mult, op1=ALU.add),
                                reads=["Tal", nm, rk], writes=[rk])
            def qcons(bk, bi, b0, n):
                P.add("act", lambda E: E.copy(out=QT[:, b0:b0 + n], in_=bk[:, 0:n]), reads=[("bank", bi)], writes=[("QT", b0 // 512)])
            proj_feat(cx, hT, 0, 2048, w, wk, 0, qcons)

            def kcons(bk, bi, b0, n):
                P.add("act", lambda E: E.activation(out=KT[:, b0:b0 + n], in_=bk[:, 0:n], func=AF.Copy, scale=KSCALE),
                      reads=[("bank", bi)], writes=[("KT", b0 // 512)])
            proj_feat(cx, hT, 0, 2048, w, wk, 128, kcons)
            flip = [0]

            def vcons(bk, bi, j):
                flip[0] ^= 1
                if flip[0]:
                    P.add("act", lambda E: E.copy(out=V[:, j, :], in_=bk[:, 0:256]), reads=[("bank", bi)], writes=[("V", j)])
                else:
                    P.add("dve", lambda E: E.tensor_copy(out=V[:, j, :], in_=bk[:, 0:256]), reads=[("bank", bi)], writes=[("V", j)])
            proj_tok(cx, hT, 16, w, wk, 256, 256, vcons)

            def gcons(bk, bi, j):
                P.add("act", lambda E: E.activation(out=G[:, j, :], in_=bk[:, 0:256], func=AF.Silu), reads=[("bank", bi)], writes=[("G", j)])
            proj_tok(cx, hT, 16, w, wk, 512, 256, gcons)
            for n0 in range(0, 16, 4):
                ret_kT_transposed(cx, KT, n0, Kzf, "Kzf", (lambda n, h=h: T["ZF"][:, h:h + 1]), "act")
                ret_kT_transposed(cx, KT, n0, Kzb, "Kzb", (lambda n, h=h: T["ZB"][:, h:h + 1]), "dve")
            for (Qx, qk, tb_, tk) in ([(Qf, "Qf", T["CFr"], "CFr"), (Qb, "Qb", T["CBr"], "CBr")] if STG >= 2 else []):
                for hf in range(2):
                    P.add("pool", lambda E, Qx=Qx, tb_=tb_, h=h, hf=hf: E.tensor_tensor(
                        out=Qx[:, hf * 1024:(hf + 1) * 1024].rearrange("p (n i) -> p n i", i=128), in0=QT3[:, hf * 8:(hf + 1) * 8, :],
                        in1=tb_[:, h:h + 1, :].to_broadcast([128, 8, 128]), op=ALU.mult),
                        reads=[("QT", 2 * hf), ("QT", 2 * hf + 1), tk], writes=[(qk, hf)])
            for n0 in (range(0, 16, 4) if STG >= 3 else []):
                bi = cx.bank()
                bk = cx.banks[bi]
                for n in range(n0, n0 + 4):
                    P.add("pe", lambda E, n=n, n0=n0, bk=bk: E.matmul(bk[:, (n - n0) * 128:(n - n0 + 1) * 128], lhsT=KT[:, n * 128:(n + 1) * 128],
                                                                        rhs=QT[:, n * 128:(n + 1) * 128], start=True, stop=True),
                          reads=[("KT", n // 4), ("QT", n // 4)], writes=[("bank", bi)])
                P.add("dve", lambda E, n0=n0, bk=bk, h=h: E.tensor_tensor(
                    out=ATs[:, n0:n0 + 4, :], in0=bk[:, :].rearrange("p (n i) -> p n i", i=128),
                    in1=T["DmT"][:, h:h + 1, :].to_broadcast([128, 4, 128]), op=ALU.mult),
                    reads=[("bank", bi), ("DmT", h)], writes=[("ATs", n0 // 4)])
            for s_ in (range(16) if STG >= 4 else []):
                nf, nb = s_, 15 - s_
                P.add("pool", lambda E, nf=nf: E.tensor_copy(out=Rfs[:, nf, :], in_=Rf[:]), reads=["Rf"], writes=[("Rfs", nf)])
                if nf < 15:
                    ret_state_step(cx, Kzf, "Kzf", V, nf, Rf, "Rf", T["cd"][:, h:h + 1])
                P.add("act", lambda E, nb=nb: E.copy(out=Rbs[:, nb, :], in_=Rb[:]), reads=["Rb"], writes=[("Rbs", nb)])
                if nb > 0:
                    ret_state_step(cx, Kzb, "Kzb", V, nb, Rb, "Rb", T["cd"][:, 8 + h:9 + h])
            for q in (range(4) if STG >= 5 else []):
                for pr in range(2):
                    bi = cx.bank()
                    bk = cx.banks[bi]
                    for c in range(2):
                        n = q * 4 + pr * 2 + c
                        o = bk[:, c * 256:(c + 1) * 256]
                        P.add("pe", lambda E, n=n, o=o: E.matmul(o, lhsT=ATs[:, n, :], rhs=V[:, n, :], start=True, stop=False),
                              reads=[("ATs", n // 4), ("V", n)], writes=[("bank", bi)])
                        P.add("pe", lambda E, n=n, o=o: E.matmul(o, lhsT=Qf[:, n * 128:(n + 1) * 128], rhs=Rfs[:, n, :], start=False, stop=False),
                              reads=[("Qf", n // 8), ("Rfs", n)], writes=[("bank", bi)])
                        P.add("pe", lambda E, n=n, o=o: E.matmul(o, lhsT=Qb[:, n * 128:(n + 1) * 128], rhs=Rbs[:, n, :], start=False, stop=True),
                              reads=[("Qb", n // 8), ("Rbs", n)], writes=[("bank", bi)])
                    P.add("act", lambda E, pr=pr, bk=bk: E.copy(out=yq[:, pr * 2:pr * 2 + 2, :].rearrange("p c v -> p (c v)"), in_=bk[:, :]),
                          reads=[("bank", bi)], writes=[("yq", pr)])
                yk = [("yq", 0), ("yq", 1)]
                P.add("dve", lambda E: E.reduce_sum(out=st[:, 0, :], in_=yq[:, :, :], axis=AX.X), reads=yk, writes=["st0"])
                P.add("act", lambda E: E.activation(out=sqq[:, :, :].rearrange("p c v -> p (c v)"), in_=yq[:, :, :].rearrange("p c v -> p (c v)"), func=AF.Square),
                      reads=yk, writes=["sqq"])
                P.add("dve", lambda E: E.reduce_sum(out=st[:, 1, :], in_=sqq[:, :, :], axis=AX.X), reads=["sqq"], writes=["st1"])
                P.add("dve", lambda E: E.tensor_scalar(out=st[:, 2, :], in0=st[:, 0, :], scalar1=1.0 / 256.0, scalar2=None, op0=ALU.mult),
                      reads=["st0"], writes=["st2"])
                P.add("dve", lambda E: E.tensor_tensor(out=st[:, 3, :], in0=st[:, 2, :], in1=st[:, 2, :], op=ALU.mult), reads=["st2"], writes=["st3"])
                P.add("dve", lambda E: E.scalar_tensor_tensor(out=st[:, 4, :], in0=st[:, 1, :], scalar=1.0 / 256.0, in1=st[:, 3, :],
                                                              op0=ALU.mult, op1=ALU.subtract), reads=["st1", "st3"], writes=["st4"])
                P.add("act", lambda E: E.activation(out=st[:, 5, :], in_=st[:, 4, :], func=AF.Sqrt, bias=epsg[:, 0:1], scale=1.0),
                      reads=["st4", "epsg"], writes=["st5"])
                P.add("dve", lambda E: E.reciprocal(out=st[:, 6, :], in_=st[:, 5, :]), reads=["st5"], writes=["st6"])
                for c in range(4):
                    P.add("dve", lambda E, c=c: E.tensor_scalar(out=yq[:, c, :], in0=yq[:, c, :], scalar1=st[:, 2, c:c + 1], scalar2=st[:, 6, c:c + 1],
                                                                op0=ALU.subtract, op1=ALU.mult),
                          reads=yk + ["st2", "st6"], writes=yk)
                P.add("pool", lambda E: E.tensor_tensor(out=yq[:, :, :], in0=yq[:, :, :], in1=GN[:, :].unsqueeze(1).to_broadcast([128, 4, 256]), op=ALU.mult),
                      reads=yk + ["GNh"], writes=yk)
                P.add("pool", lambda E, q=q: E.tensor_tensor(out=Z[:, q * 4:(q + 1) * 4, :], in0=yq[:, :, :], in1=G[:, q * 4:(q + 1) * 4, :], op=ALU.mult),
                      reads=yk + [("G", q * 4 + c) for c in range(4)], writes=[("Z", q * 4 + c) for c in range(4)])
            for mm in (range(2) if STG >= 6 else []):
                outproj_chunk(cx, xT, Z, "Z", mm * 128, 16, (lambda o, mm=mm: Wo[:, mm, o * 128:(o + 1) * 128]), "Wo1", OTm, cx.ident)


SHAPES_A = dict(xT=[1024, 2048], xh=[1024, 512], kval=[128, 20], win=[1024, 2304], gqk=[128, 4], sink=[128, 8],
                g_attn=[1024], gqa_tab=[128, 3072], na_tab=[5, 128, 8, 14, 64], wout=[1024, 1024], ident=[128, 128],
                g_mlp=[1024], w1=[1024, 4096], w2=[4096, 1024])


def mlp_scope(cx, xT, g_dram, w1, w2, w1k=(), w2k=()):
    P = cx.P
    with P.scope():
        gm = P.sbuf([128, 8], F32, "gm")
        P.dma("sp", gm[:], g_dram.rearrange("(c p) -> p c", p=128), writes=["gains"], stream="c_gm", allow_slow_non_contiguous=True)
        scr = dict(
            sq=P.sbuf([128, 8, 256], F32, name="sq"), acc=P.sbuf([128, 512], F32, name="acc"),
            rstd=P.sbuf([128, 512], F32, name="rstd"),
            W2=P.sbuf([128, 32, 1024], BF16, name="W2"),
            W1b=[P.sbuf([128, 8, 512], BF16, name=f"W1b{i}") for i in range(2)],
            aT=P.sbuf([128, 32, 512], BF16, name="aT"), hT=P.sbuf([128, 8, 512], BF16, name="hTm"),
            rl=[P.sbuf([128, 512], F32, name=f"rl{i}") for i in range(2)],
        )
        mlp_T(cx, 0, xT, w1, w2, lambda c: gm[:, c:c + 1], scr, w1k, w2k)


def build_A(do_mixer=True, do_mlp=True):
    nc = bass.Bass("TRN2", target_bir_lowering=False)
    D = {k: nc.dram_tensor(k, s, F32, kind="ExternalInput").ap() for k, s in SHAPES_A.items()}
    yout = nc.dram_tensor("yT", [1024, 2048], F32, kind="ExternalOutput").ap()
    P = Prog(nc)
    cx = Ctx(P, 2048, ident_dram=D["ident"])
    xT = P.sbuf([128, 8, 2048], F32, name="xT_sb")
    xv = D["xT"].rearrange("(c p) t -> p c t", p=128)
    for tb in range(4):
        P.dma("sp", xT[:, :, tb * 512:(tb + 1) * 512], xv[:, :, tb * 512:(tb + 1) * 512], writes=[("xT", tb)], stream=f"xin{tb}")
    if do_mixer:
        even_mixer(cx, xT, D)
    if do_mlp:
        mlp_scope(cx, xT, D["g_mlp"], D["w1"], D["w2"])
    yv = yout.rearrange("(c p) t -> p c t", p=128)
    for tb in range(4):
        P.dma("sp", yv[:, :, tb * 512:(tb + 1) * 512], xT[:, :, tb * 512:(tb + 1) * 512], reads=[("xT", tb)], stream="yout")
    P.emit(final_waits=["yout"])
    print("ops:", {e: len(P.ops[e]) for e in P.ENGS}, "signals:", P.sig_counts, flush=True)
    return nc


SHAPES_F = dict(SHAPES_A, g_ret=[1024], win_o=[1024, 6144], dec=[128, 16], rcols=[128, 4], Pf=[128, 128], Pb=[128, 128],
                wout_o=[2048, 1024], gn_rep=[128, 2048], rcw=[128, 32], irow=[128, 256], e1=[128, 32], g_mlp1=[1024], w1b=[1024, 4096], w2b=[4096, 1024])


def build_F():
    nc = bass.Bass("TRN2", target_bir_lowering=False)
    D = {k: nc.dram_tensor(k, s, F32, kind="ExternalInput").ap() for k, s in SHAPES_F.items()}
    yout = nc.dram_tensor("yT", [1024, 2048], F32, kind="ExternalOutput").ap()
    tin = nc.dram_tensor("t_in", [128, 4096], F32)
    tall = nc.dram_tensor("t_all", [1024, 4096], F32)
    P = Prog(nc)
    cx = Ctx(P, 2048, ident_dram=D["ident"])
    xT = P.sbuf([128, 8, 2048], F32, name="xT_sb")
    xv = D["xT"].rearrange("(c p) t -> p c t", p=128)
    for tb in range(4):
        P.dma("sp", xT[:, :, tb * 512:(tb + 1) * 512], xv[:, :, tb * 512:(tb + 1) * 512], writes=[("xT", tb)], stream=f"xin{tb}")
    def precast(name, src, rows, cols, prow):
        t = nc.dram_tensor("bf_" + name, [rows, cols], BF16)
        keys = []
        for i, r0 in enumerate(range(0, rows, prow)):
            k = ("wb", name, i)
            keys.append(k)
            P.dma("pool", t.ap()[r0:r0 + prow, :], src[r0:r0 + prow, :], writes=[k], stream="pc_" + name)
        return t.ap(), keys
    D["win_k"] = [("wb", "win", i) for i in range(4)]
    D["wout_k"] = [("wb", "wout", i) for i in range(2)]
    twin = nc.dram_tensor("bf_win", [1024, 2304], BF16)
    twout = nc.dram_tensor("bf_wout", [1024, 1024], BF16)
    D["win_bf"] = twin.ap()
    D["wout_bf"] = twout.ap()
    W = {}
    D["wino_k"] = [("wb", "wino", i) for i in range(8)]
    D["wouto_k"] = [("wb", "wouto", i) for i in range(4)]
    W["w2ak"] = [("wb", "w2a", i) for i in range(8)]
    W["w1ak"] = [("wb", "w1a", i) for i in range(8)]
    W["w2bk"] = [("wb", "w2b", i) for i in range(8)]
    W["w1bk"] = [("wb", "w1b", i) for i in range(8)]
    tens = {}
    for nm, shp in [("w2a", [4096, 1024]), ("w1a", [1024, 4096]), ("wino", [1024, 6144]), ("wouto", [2048, 1024]), ("w2b", [4096, 1024]), ("w1b", [1024, 4096])]:
        tens[nm] = nc.dram_tensor("bf_" + nm, shp, BF16)

    def pc2(name, src, rows, prow):
        t = tens[name]
        for i, r0 in enumerate(range(0, rows, prow)):
            P.dma("pool", t.ap()[r0:r0 + prow, :], src[r0:r0 + prow, :], writes=[("wb", name, i)], stream="pc_" + name)
    D["wino_bf"] = tens["wino"].ap()
    D["wouto_bf"] = tens["wouto"].ap()
    w2a, w1a, w2b, w1b = tens["w2a"].ap(), tens["w1a"].ap(), tens["w2b"].ap(), tens["w1b"].ap()
    w2ak, w1ak, w2bk, w1bk = W["w2ak"], W["w1ak"], W["w2bk"], W["w1bk"]
    HOOKS = False
    hk = {
        "h1": lambda: pc2("w2a", D["w2"], 4096, 512),
        "h2": lambda: pc2("w1a", D["w1"], 1024, 128),
        "h3": lambda: pc2("wino", D["win_o"], 1024, 128),
        "h4": lambda: (pc2("wouto", D["wout_o"], 2048, 512), pc2("w2b", D["w2b"], 4096, 512)),
        "h5": lambda: pc2("w1b", D["w1b"], 1024, 128),
    }
    def h0():
        early = [("xT", tb) for tb in range(4)] + ["xh", "mst", "gqk", "esink", "kval", "gains", ("wAall",), ("Wout", 0), ("Wout", 1)]
        for i, r0 in enumerate(range(0, 1024, 256)):
            P.dma("pool", twin.ap()[r0:r0 + 256, :], D["win"][r0:r0 + 256, :], reads=(early if i == 0 else []), writes=[("wb", "win", i)], stream="pc_win")
        for k in ["h1", "h2", "h3", "h4", "h5"]:
            hk[k]()
    cx.hooks = {"h0": h0}
    even_mixer(cx, xT, D)
    mlp_scope(cx, xT, D["g_mlp"], w1a, w2a, w1ak, w2ak)
    with P.scope():
        hT = P.sbuf([128, 8, 2048], BF16, "hT1")
        T = ret_alloc_tables(P, True)
        ret_prologue(cx, xT, D, hT, T)
        ret_pass1(cx, hT, T, D, tin.ap().rearrange("p (d h v) -> p d h v", d=2, h=8))
        P.add("pool", lambda E: E.collective_compute("AllGather", ALU.bypass, replica_groups=[list(range(8))],
                                                     ins=[tin.ap().opt()], outs=[tall.ap().opt()]),
              reads=[("ToutD", d, h) for d in range(2) for h in range(8)], writes=["TallD"], dma_stream="cc", dma_inc=1)
        import os
        if os.environ.get("SKIP_MAIN") != "1":
            ret_main(cx, xT, hT, T, D, tall.ap().rearrange("(c p) (d h v) -> c p d h v", p=128, d=2, h=8))
    mlp_scope(cx, xT, D["g_mlp1"], w1b, w2b, w1bk, w2bk)
    yv = yout.rearrange("(c p) t -> p c t", p=128)
    for tb in range(4):
        P.dma("sp", yv[:, :, tb * 512:(tb + 1) * 512], xT[:, :, tb * 512:(tb + 1) * 512], reads=[("xT", tb)], stream="yout")
    P.emit(final_waits=["yout"])
    print("F ops:", {e: len(P.ops[e]) for e in P.ENGS}, "signals:", P.sig_counts, flush=True)
    return nc


import numpy as np

NCORE = 8
TPC = 2048


def alibi_slopes(n):
    return (2.0 ** (-(8.0 / n) * (np.arange(n, dtype=np.float32) + 1.0))).astype(np.float32)


def gqa_table():
    k = np.arange(128)[:, None, None, None]
    rp = np.arange(3)[None, :, None, None]
    h = np.arange(8)[None, None, :, None]
    q = np.arange(128)[None, None, None, :]
    rel = (rp - 1) * 128 + k - q
    sl = alibi_slopes(8)[h]
    a = np.where(np.abs(rel) <= 128, -sl * np.abs(rel).astype(np.float32), np.float32(-30000.0)).astype(np.float32)
    a = np.broadcast_to(a, (128, 3, 8, 128))
    return np.ascontiguousarray(a).reshape(128, 3072)


def na_table(rpb, core):
    out = np.empty((5, 128, 8, 14, 64), np.float32)
    p = np.arange(128)
    a = (p // 64)[:, None, None]
    kc = (p % 64)[:, None, None]
    sl = np.arange(14)[None, :, None]
    qc = np.arange(64)[None, None, :]
    u = sl // 2 - 1
    b = sl % 2
    dr = 2 * u - 4 + a - b
    cstart = np.clip(qc - 8, 0, 64 - 16)
    colv = (kc >= cstart) & (kc < cstart + 16)
    cidx = np.clip(kc - qc + 15, 0, 30)
    tmap = {0: None, 1: 0, 2: 1, 3: 14, 4: 15}
    for v in range(5):
        t = tmap[v]
        if t is None:
            r = 100 + b
        else:
            r = core * 32 + 2 * t + b
        rstart = np.clip(r - 4, 0, 256 - 8)
        kr = r + dr
        rowv = (kr >= rstart) & (kr < rstart + 8) & (np.abs(dr) <= 7)
        ridx = np.clip(dr + 7, 0, 14)
        valid = np.broadcast_to(rowv & colv, (128, 14, 64))
        ri = np.broadcast_to(ridx, (128, 14, 64))
        ci = np.broadcast_to(cidx, (128, 14, 64))
        for h in range(8):
            g = rpb[h][ri, ci]
            out[v, :, h] = np.where(valid, g, np.float32(-30000.0))
    return out


def prep_layer0(inp, core):
    x = inp["x"][0]
    s0 = core * TPC
    d = {}
    d["xT"] = np.ascontiguousarray(x[s0:s0 + TPC].T)
    left = x[s0 - 256:s0] if core > 0 else np.zeros((256, 1024), np.float32)
    right = x[s0 + TPC:s0 + TPC + 256] if core < NCORE - 1 else np.zeros((256, 1024), np.float32)
    d["xh"] = np.ascontiguousarray(np.concatenate([left, right], 0).T)
    kv = np.ones((128, 20), np.float32)
    if core == 0:
        kv[:, 0:2] = 0
    if core == NCORE - 1:
        kv[:, 18:20] = 0
    d["kval"] = kv
    w = inp["w_in_e"][0]
    qa = w[:, 0:512].reshape(1024, 8, 64)
    perm = [0, 4, 1, 5, 2, 6, 3, 7]
    qa = qa[:, perm, :].reshape(1024, 512)
    d["win"] = np.ascontiguousarray(np.concatenate([qa, w[:, 512:]], 1))
    g = np.stack([np.tile(inp["q_norm_a"][0], 2), np.tile(inp["k_norm_a"][0], 2),
                  np.tile(inp["q_norm_b"][0], 2), np.tile(inp["k_norm_b"][0], 2)], 1).astype(np.float32)
    d["gqk"] = np.ascontiguousarray(g)
    d["sink"] = np.ascontiguousarray(np.broadcast_to(inp["sink_a"][0][None, :], (128, 8)).astype(np.float32))
    d["g_attn"] = np.ascontiguousarray(inp["attn_norm_e"][0])
    d["gqa_tab"] = gqa_table()
    d["na_tab"] = na_table(inp["rpb_b"][0], core)
    d["wout"] = np.ascontiguousarray(inp["w_out_e"][0])
    d["ident"] = np.eye(128, dtype=np.float32)
    d["g_mlp"] = np.ascontiguousarray(inp["mlp_norm"][0])
    d["w1"] = np.ascontiguousarray(inp["w_mlp_in"][0])
    d["w2"] = np.ascontiguousarray(inp["w_mlp_out"][0])
    return d


def prep_layer1_common(inp, x1T):
    d = {}
    d["xT"] = np.ascontiguousarray(x1T)
    d["g_ret"] = np.ascontiguousarray(inp["ret_norm_o"][0])
    d["win_o"] = np.ascontiguousarray(inp["w_in_o"][0])
    dec = np.concatenate([inp["decay_fwd_o"][0], inp["decay_bwd_o"][0]]).astype(np.float32)
    d["dec"] = np.ascontiguousarray(np.broadcast_to(dec[None, :], (128, 16)))
    i = np.arange(128, dtype=np.float32)
    d["rcols"] = np.ascontiguousarray(np.stack([i + 1, 128 - i, 127 - i, i], 1).astype(np.float32))
    jj = i[:, None]
    ii = i[None, :]
    d["Pf"] = np.ascontiguousarray(np.maximum(ii - jj, 0).astype(np.float32))
    d["Pb"] = np.ascontiguousarray(np.maximum(jj - ii, 0).astype(np.float32))
    d["ident"] = np.eye(128, dtype=np.float32)
    row = np.concatenate([i + 1, 128 - i]).astype(np.float32)
    d["irow"] = np.ascontiguousarray(np.broadcast_to(row[None, :], (128, 256)))
    n = np.arange(16, dtype=np.float32)[None, :]
    d["e1"] = np.ascontiguousarray(np.concatenate([127 - jj + 128 * (15 - n), jj + 128 * n], 1).astype(np.float32))
    return d


def prep_layer1_C(inp, x1T, Tall, core):
    d = prep_layer1_common(inp, x1T)
    d["wout_o"] = np.ascontiguousarray(inp["w_out_o"][0])
    d["gn_rep"] = np.ascontiguousarray(np.broadcast_to(inp["ret_gn_o"][0][None, :], (128, 2048)).astype(np.float32))
    d["Tall"] = Tall
    cp = np.arange(8)
    ef = np.where(cp < core, 2048.0 * (core - cp - 1), 0.0)
    mf = (cp < core).astype(np.float32)
    eb = np.where(cp > core, 2048.0 * (cp - core - 1), 0.0)
    mb = (cp > core).astype(np.float32)
    row = np.concatenate([ef, mf, eb, mb]).astype(np.float32)
    d["rcw"] = np.ascontiguousarray(np.broadcast_to(row[None, :], (128, 32)))
    d["g_mlp"] = np.ascontiguousarray(inp["mlp_norm"][1])
    d["w1"] = np.ascontiguousarray(inp["w_mlp_in"][1])
    d["w2"] = np.ascontiguousarray(inp["w_mlp_out"][1])
    return d


def prep_fused(inp, core):
    d = prep_layer0(inp, core)
    d1 = prep_layer1_C(inp, d["xT"], None, core)
    for k in ["g_ret", "win_o", "dec", "rcols", "Pf", "Pb", "wout_o", "gn_rep", "rcw", "irow", "e1"]:
        d[k] = d1[k]
    d["g_mlp1"] = d1["g_mlp"]
    d["w1b"] = d1["w1"]
    d["w2b"] = d1["w2"]
    return d

from concourse.bass_utils import run_bass_kernel_spmd

_CACHE = {}


def _prog(name, fn):
    if name not in _CACHE:
        _CACHE[name] = fn()
    return _CACHE[name]


def kernel(**inp):
    inp = {k: np.asarray(v) for k, v in inp.items()}
    cores = list(range(8))
    nc = build_F()
    res = run_bass_kernel_spmd(nc, [prep_fused(inp, c) for c in cores], core_ids=cores)
    y = np.concatenate([np.asarray(res.results[c]["yT"]).T for c in cores], 0)
    return np.ascontiguousarray(y[None].astype(np.float32))
```
